# Optimizing a Trainium2 kernel written in Bass

```python
import jax, jax.numpy as jnp
from jax import lax
import numpy as np

D_MODEL = 2048
BATCH = 8
SEQ = 2048
DEPTH = 2
DEC_BATCH = 128
DEC_SEQ = 4
PAST_LEN = 8192
PAGE_SIZE = 128

N_A_LAYERS = DEPTH // 2
N_B_LAYERS = DEPTH - N_A_LAYERS
D_RNN = D_MODEL
N_LRU_BLOCKS = 16
LRU_BLOCK = D_RNN // N_LRU_BLOCKS
CONV_W = 4
LRU_C = 8.0
N_HEADS = 32
N_KV_HEADS = 4
HEAD_DIM = 64
GROUP = N_HEADS // N_KV_HEADS
WINDOW = 128
ATTN_BLOCK = WINDOW
D_FF = 4 * D_MODEL
EPS = 1e-6

kernel_name = 'yoco_hawk_swa_sink_decoder_step'

F32 = jnp.float32


def rmsnorm(x, g):
    xf = x.astype(F32)
    y = xf * lax.rsqrt(jnp.mean(xf * xf, axis=-1, keepdims=True) + EPS)
    return (y * g.astype(F32)).astype(x.dtype)


def causal_conv(x, buf, w, b):
    T = x.shape[1]
    xp = jnp.concatenate([buf.astype(x.dtype), x], axis=1)
    y = b + w[0] * xp[:, 0:T]
    for j in range(1, CONV_W):
        y = y + w[j] * xp[:, j:j + T]
    return y, xp[:, xp.shape[1] - (CONV_W - 1):]


def block_diag(x, w, b):
    xb = x.reshape(x.shape[:-1] + (N_LRU_BLOCKS, LRU_BLOCK))
    y = jnp.einsum('btnc,ncd->btnd', xb, w) + b
    return y.reshape(x.shape)


def rg_lru(x, h0, wa, ba, wx, bx, lam):
    r = jax.nn.sigmoid(block_diag(x, wa, ba).astype(F32))
    i = jax.nn.sigmoid(block_diag(x, wx, bx).astype(F32))
    log_a = -LRU_C * r * jax.nn.softplus(-lam.astype(F32))
    a = jnp.exp(log_a)
    u = jnp.sqrt(-jnp.expm1(2.0 * log_a)) * (i * x.astype(F32))

    def step(h, au):
        a_t, u_t = au
        h = a_t * h + u_t
        return h, h

    hT, hs = lax.scan(step, h0.astype(F32), (jnp.swapaxes(a, 0, 1), jnp.swapaxes(u, 0, 1)))
    return jnp.swapaxes(hs, 0, 1).astype(x.dtype), hT.astype(h0.dtype)


def recurrent_block(xn, conv_buf, h0, w_in, conv_w, conv_b, wa, ba, wx, bx, lam, w_out):
    gate, xr = jnp.split(xn @ w_in, 2, axis=-1)
    xr, new_buf = causal_conv(xr, conv_buf, conv_w, conv_b)
    hs, hT = rg_lru(xr, h0, wa, ba, wx, bx, lam)
    y = (jax.nn.gelu(gate) * hs) @ w_out
    return y, new_buf, hT


def sink_attention(q, k, v, q_pos, k_pos, sinks):
    s = jnp.einsum('bnqhgd,bnkhd->bnhgqk', q, k, preferred_element_type=F32) * (HEAD_DIM ** -0.5)
    dq = q_pos[:, :, None] - k_pos[:, None, :]
    allowed = (dq >= 0) & (dq < WINDOW) & (k_pos[:, None, :] >= 0)
    s = jnp.where(allowed[None, :, None, None], s, -jnp.inf)
    sink = sinks.astype(F32).reshape(1, 1, N_KV_HEADS, GROUP, 1, 1)
    m = jnp.maximum(jnp.max(s, axis=-1, keepdims=True), sink)
    p = jnp.exp(s - m)
    p = p / (jnp.sum(p, axis=-1, keepdims=True) + jnp.exp(sink - m))
    return jnp.einsum('bnhgqk,bnkhd->bnqhgd', p.astype(v.dtype), v)


def attention_prompt(xn, k, v, w_q, sinks, w_o):
    B, T, _ = xn.shape
    nb = T // ATTN_BLOCK
    q = (xn @ w_q).reshape(B, nb, ATTN_BLOCK, N_KV_HEADS, GROUP, HEAD_DIM)
    kb = k.reshape(B, nb, ATTN_BLOCK, N_KV_HEADS, HEAD_DIM)
    vb = v.reshape(B, nb, ATTN_BLOCK, N_KV_HEADS, HEAD_DIM)
    pad = ((0, 0), (1, 0), (0, 0), (0, 0), (0, 0))
    kk = jnp.concatenate([jnp.pad(kb[:, :-1], pad), kb], axis=2)
    vv = jnp.concatenate([jnp.pad(vb[:, :-1], pad), vb], axis=2)
    pos = jnp.arange(T, dtype=jnp.int32).reshape(nb, ATTN_BLOCK)
    k_pos = jnp.concatenate([pos - ATTN_BLOCK, pos], axis=1)
    o = sink_attention(q, kk, vv, pos, k_pos, sinks)
    return o.reshape(B, T, N_HEADS * HEAD_DIM) @ w_o


def attention_sample(xn, k_new, v_new, k_cache, v_cache, w_q, sinks, w_o):
    B, T, _ = xn.shape
    n_buf = k_cache.shape[1]
    q = (xn @ w_q).reshape(B, 1, T, N_KV_HEADS, GROUP, HEAD_DIM)
    kk = jnp.concatenate([k_cache.astype(k_new.dtype), k_new], axis=1)[:, None]
    vv = jnp.concatenate([v_cache.astype(v_new.dtype), v_new], axis=1)[:, None]
    q_pos = PAST_LEN + jnp.arange(T, dtype=jnp.int32)
    k_pos = jnp.concatenate([PAST_LEN - n_buf + jnp.arange(n_buf, dtype=jnp.int32), q_pos])
    o = sink_attention(q, kk, vv, q_pos[None], k_pos[None], sinks)
    return o.reshape(B, T, N_HEADS * HEAD_DIM) @ w_o


def sq_relu_mlp(xn, w_up, w_down):
    return jnp.square(jax.nn.relu(xn @ w_up)) @ w_down


def trunk(x, conv_state, h_state, k_cache, v_cache, norm_mix, norm_mlp, rec_w_in, rec_conv_w, rec_conv_b,
          rec_gate_a_w, rec_gate_a_b, rec_gate_x_w, rec_gate_x_b, rec_lambda, rec_w_out, kv_norm, w_kv,
          attn_w_q, attn_sinks, attn_w_o, mlp_w_up, mlp_w_down, final_norm):
    B, T, _ = x.shape
    h = x
    new_conv, new_h = [], []
    k = v = None
    for layer in range(DEPTH):
        xn = rmsnorm(h, norm_mix[layer])
        if layer < N_A_LAYERS:
            y, cb, hT = recurrent_block(xn, conv_state[:, layer], h_state[:, layer], rec_w_in[layer],
                                        rec_conv_w[layer], rec_conv_b[layer], rec_gate_a_w[layer],
                                        rec_gate_a_b[layer], rec_gate_x_w[layer], rec_gate_x_b[layer],
                                        rec_lambda[layer], rec_w_out[layer])
            new_conv.append(cb)
            new_h.append(hT)
        else:
            j = layer - N_A_LAYERS
            if k_cache is None:
                y = attention_prompt(xn, k, v, attn_w_q[j], attn_sinks[j], attn_w_o[j])
            else:
                y = attention_sample(xn, k, v, k_cache, v_cache, attn_w_q[j], attn_sinks[j], attn_w_o[j])
        h = h + y
        h = h + sq_relu_mlp(rmsnorm(h, norm_mlp[layer]), mlp_w_up[layer], mlp_w_down[layer])
        if layer == N_A_LAYERS - 1:
            k, v = jnp.split(rmsnorm(h, kv_norm) @ w_kv, 2, axis=-1)
            k = k.reshape(B, T, N_KV_HEADS, HEAD_DIM)
            v = v.reshape(B, T, N_KV_HEADS, HEAD_DIM)
    out = rmsnorm(h, final_norm)
    return out, jnp.stack(new_conv, axis=1), jnp.stack(new_h, axis=1), k, v


def setup_inputs(seed: int = 0) -> dict:
    key = jax.random.key(seed)
    ks = jax.random.split(key, 32)
    nrm = lambda k, shape, scale: jax.random.normal(k, shape, F32) * scale
    p_a = jax.random.uniform(ks[14], (N_A_LAYERS, D_RNN), F32, 0.9, 0.999)
    return {
        'x_prompt': nrm(ks[0], (BATCH, SEQ, D_MODEL), 1.0),
        'x_sample': nrm(ks[1], (DEC_BATCH, DEC_SEQ, D_MODEL), 1.0),
        'state_conv': nrm(ks[2], (DEC_BATCH, N_A_LAYERS, CONV_W - 1, D_RNN), 1.0),
        'state_h': nrm(ks[3], (DEC_BATCH, N_A_LAYERS, D_RNN), 0.5),
        'cache_k': nrm(ks[4], (DEC_BATCH, WINDOW, N_KV_HEADS, HEAD_DIM), 1.0),
        'cache_v': nrm(ks[5], (DEC_BATCH, WINDOW, N_KV_HEADS, HEAD_DIM), 1.0),
        'norm_mix': 1.0 + nrm(ks[6], (DEPTH, D_MODEL), 0.05),
        'norm_mlp': 1.0 + nrm(ks[7], (DEPTH, D_MODEL), 0.05),
        'rec_w_in': nrm(ks[8], (N_A_LAYERS, D_MODEL, 2 * D_RNN), D_MODEL ** -0.5),
        'rec_conv_w': nrm(ks[9], (N_A_LAYERS, CONV_W, D_RNN), CONV_W ** -0.5),
        'rec_conv_b': nrm(ks[10], (N_A_LAYERS, D_RNN), 0.01),
        'rec_gate_a_w': nrm(ks[11], (N_A_LAYERS, N_LRU_BLOCKS, LRU_BLOCK, LRU_BLOCK), LRU_BLOCK ** -0.5),
        'rec_gate_a_b': nrm(ks[12], (N_A_LAYERS, N_LRU_BLOCKS, LRU_BLOCK), 0.01),
        'rec_gate_x_w': nrm(ks[13], (N_A_LAYERS, N_LRU_BLOCKS, LRU_BLOCK, LRU_BLOCK), LRU_BLOCK ** -0.5),
        'rec_gate_x_b': nrm(ks[15], (N_A_LAYERS, N_LRU_BLOCKS, LRU_BLOCK), 0.01),
        'rec_lambda': jnp.log(p_a) - jnp.log1p(-p_a),
        'rec_w_out': nrm(ks[16], (N_A_LAYERS, D_RNN, D_MODEL), D_RNN ** -0.5),
        'kv_norm': 1.0 + nrm(ks[17], (D_MODEL,), 0.05),
        'w_kv': nrm(ks[18], (D_MODEL, 2 * N_KV_HEADS * HEAD_DIM), D_MODEL ** -0.5),
        'attn_w_q': nrm(ks[19], (N_B_LAYERS, D_MODEL, N_HEADS * HEAD_DIM), D_MODEL ** -0.5),
        'attn_sinks': nrm(ks[20], (N_B_LAYERS, N_HEADS), 0.5),
        'attn_w_o': nrm(ks[21], (N_B_LAYERS, N_HEADS * HEAD_DIM, D_MODEL), (N_HEADS * HEAD_DIM) ** -0.5),
        'mlp_w_up': nrm(ks[22], (DEPTH, D_MODEL, D_FF), D_MODEL ** -0.5),
        'mlp_w_down': nrm(ks[23], (DEPTH, D_FF, D_MODEL), D_FF ** -0.5),
        'final_norm': 1.0 + nrm(ks[24], (D_MODEL,), 0.05),
    }


def reference(x_prompt, x_sample, state_conv, state_h, cache_k, cache_v, norm_mix, norm_mlp, rec_w_in,
              rec_conv_w, rec_conv_b, rec_gate_a_w, rec_gate_a_b, rec_gate_x_w, rec_gate_x_b, rec_lambda,
              rec_w_out, kv_norm, w_kv, attn_w_q, attn_sinks, attn_w_o, mlp_w_up, mlp_w_down, final_norm):
    weights = (norm_mix, norm_mlp, rec_w_in, rec_conv_w, rec_conv_b, rec_gate_a_w, rec_gate_a_b,
               rec_gate_x_w, rec_gate_x_b, rec_lambda, rec_w_out, kv_norm, w_kv, attn_w_q, attn_sinks,
               attn_w_o, mlp_w_up, mlp_w_down, final_norm)
    B, T, _ = x_prompt.shape
    conv0 = jnp.zeros((B, N_A_LAYERS, CONV_W - 1, D_RNN), x_prompt.dtype)
    h0 = jnp.zeros((B, N_A_LAYERS, D_RNN), state_h.dtype)
    y_prompt, conv_p, h_p, k_p, v_p = trunk(x_prompt, conv0, h0, None, None, *weights)
    keep = min(WINDOW, T)
    new_k_prompt = k_p[:, T - keep:]
    new_v_prompt = v_p[:, T - keep:]
    y_sample, conv_s, h_s, k_s, v_s = trunk(x_sample, state_conv, state_h, cache_k, cache_v, *weights)
    return (y_prompt, y_sample, conv_p, h_p, new_k_prompt, new_v_prompt, conv_s, h_s, k_s, v_s)
```

```python
import contextlib
import numpy as np
import concourse.bass as bass
import concourse.mybir as mybir
from concourse.bass_utils import run_bass_kernel_spmd

F32 = mybir.dt.float32
BF16 = mybir.dt.bfloat16
AF = mybir.ActivationFunctionType
ALU = mybir.AluOpType

ENGS = ("pe", "act", "dve", "pool", "sp")
NCORES = 8
D = 2048
NCH = 16
DFF = 8192
EPS = 1e-6
NEG = -30000.0
NSLOT = 3


class Sem:
    def __init__(self, handle, name):
        self.h = handle
        self.name = name
        self.count = 0


class Prog:
    def __init__(self, nc, stack):
        self.nc = nc
        self.stack = stack
        self.q = {e: [] for e in ENGS}
        self.esem = {e: self.new_sem("prog_" + e) for e in ENGS}
        self.seen = {e: {} for e in ENGS}
        self.res = {}
        self.dma_sems = []

    def new_sem(self, name):
        return Sem(self.stack.enter_context(self.nc.semaphore(name)), name)

    def new_dma_sem(self, name):
        s = self.new_sem(name)
        self.dma_sems.append(s)
        return s

    def _deps(self, eng, reads, writes, is_dma):
        need = {}

        def add(tok, raw):
            if tok is None:
                return
            sem, val, teng = tok
            if (not is_dma) and teng == eng:
                if not raw or eng == "pe":
                    return
            if need.get(sem, 0) < val:
                need[sem] = val

        for k in reads:
            st = self.res.get(k)
            if st is not None:
                add(st[0], True)
        for k in writes:
            st = self.res.get(k)
            if st is not None:
                add(st[0], False)
                for t in st[1]:
                    add(t, False)
        waits = []
        seen = self.seen[eng]
        for sem, val in need.items():
            if seen.get(sem, 0) >= val:
                continue
            seen[sem] = val
            waits.append((sem, val))
        return waits

    def _commit(self, tok, reads, writes):
        for k in reads:
            st = self.res.setdefault(k, [None, []])
            st[1].append(tok)
        for k in writes:
            self.res[k] = [tok, []]

    def op(self, eng, fn, reads=(), writes=()):
        reads = list(reads)
        writes = list(writes)
        waits = self._deps(eng, reads, writes, False)
        sem = self.esem[eng]
        sem.count += 1
        tok = (sem, sem.count, eng)
        self.q[eng].append((waits, fn, sem, 1))
        self._commit(tok, reads, writes)
        return tok

    def dma(self, eng, out, in_, sem, reads=(), writes=()):
        reads = list(reads)
        writes = list(writes)
        waits = self._deps(eng, reads, writes, True)
        sem.count += 16
        tok = (sem, sem.count, None)
        self.q[eng].append((waits, lambda e: e.dma_start(out=out, in_=in_), sem, 16))
        self._commit(tok, reads, writes)
        return tok

    def dma_group(self, eng, pairs, sem, reads=(), writes=()):
        reads = list(reads)
        writes = list(writes)
        waits = self._deps(eng, reads, writes, True)
        for i, (out, in_) in enumerate(pairs):
            sem.count += 16
            self.q[eng].append((waits if i == 0 else [], (lambda e, out=out, in_=in_: e.dma_start(out=out, in_=in_)), sem, 16))
        tok = (sem, sem.count, None)
        self._commit(tok, reads, writes)
        return tok

    def finalize(self):
        waits = [(s, s.count) for s in self.dma_sems if s.count > 0]
        self.q["sp"].append((waits, None, None, 0))

    def emit(self):
        nc = self.nc
        names = {"pe": "tensor", "act": "scalar", "dve": "vector", "pool": "gpsimd", "sp": "sync"}
        with nc.Block() as block:
            for eng in ENGS:
                items = self.q[eng]
                if not items:
                    continue

                def body(e, items=items):
                    for waits, fn, sem, inc in items:
                        for s, v in waits:
                            e.wait_ge(s.h, v)
                        if fn is None:
                            continue
                        ins = fn(e)
                        ins.then_inc(sem.h, inc)

                getattr(block, names[eng])(body)


V_GMIX0, V_GMLP0, V_GKV, V_GMIX1, V_GMLP1, V_GFIN, V_CW0, V_CW1, V_CW2, V_CW3, V_CB, V_BA, V_BX, V_LAM, V_SINK = range(15)
NVEC = 15


def build_program(passes=(0, 1, 2, 3, 4), phase_limit=99, dump_h=False):
    nc = bass.Bass("TRN2", target_bir_lowering=False)

    def din(name, shape):
        return nc.dram_tensor(name, shape, F32, kind="ExternalInput").ap()

    def dout(name, shape):
        return nc.dram_tensor(name, shape, F32, kind="ExternalOutput").ap()

    xp = din("xp", [2048, D])
    xs = din("xs", [64, D])
    sconv = din("sconv", [48, D])
    sh = din("sh", [16, D])
    ck = din("ck", [16, 128, 256])
    cv = din("cv", [16, 128, 256])
    vecs_d = din("vecs", [128, NVEC * 16])
    w_in = din("w_in", [D, 2 * D])
    w_ga = din("w_ga", [16, 128, 128])
    w_gx = din("w_gx", [16, 128, 128])
    w_out = din("w_out", [D, D])
    w_kv = din("w_kv", [D, 512])
    w_q = din("w_q", [D, D])
    w_o = din("w_o", [D, D])
    w_up = [din("w_up0", [D, DFF]), din("w_up1", [D, DFF])]
    w_dn = [din("w_dn0", [DFF, D]), din("w_dn1", [DFF, D])]

    yp = dout("yp", [2048, D])
    ys = dout("ys", [64, D])
    o_convp = dout("o_convp", [3, D])
    o_hp = dout("o_hp", [1, D])
    o_kp = dout("o_kp", [128, 256])
    o_vp = dout("o_vp", [128, 256])
    o_convs = dout("o_convs", [16, 3, D])
    o_hs = dout("o_hs", [16, D])
    o_ks = dout("o_ks", [64, 256])
    o_vs = dout("o_vs", [64, 256])
    wscr = nc.dram_tensor("wscr", [85, 128, 8192], BF16, kind="Internal").ap()
    dbg_h = dout("dbg_h", [128, NCH, 512]) if dump_h else None
    dbg_u = dout("dbg_u", [128, 16384]) if dump_h else None
    dbg_k = nc.dram_tensor("dbg_k", [128, 4 * 2 * 640], BF16, kind="ExternalOutput").ap() if dump_h else None
    dbg_v = nc.dram_tensor("dbg_v", [128, 5 * 4 * 192], BF16, kind="ExternalOutput").ap() if dump_h else None

    with contextlib.ExitStack() as st:
        P = Prog(nc, st)

        def sb(name, shape, dt):
            return st.enter_context(nc.sbuf_tensor("sb_" + name, shape, dt))

        hres = sb("hres", [128, NCH, 512], F32)
        xn = sb("xn", [128, NCH, 512], BF16)
        U = sb("U", [128, 16384], F32)
        wslot = [sb(f"wslot{i}", [128, 8192], BF16) for i in range(NSLOT)]
        wgate = sb("wgate", [128, 2, 16, 128], BF16)
        K2T = sb("K2T", [128, 4, 2, 640], BF16)
        Vpad = sb("Vpad", [128, 5, 4, 192], BF16)
        vecs = sb("vecs", [128, NVEC * 16], F32)
        dvec = sb("dvec", [128, 5 * 16], F32)
        ident = sb("ident", [128, 128], F32)
        identb = sb("identb", [128, 128], BF16)
        onesb = sb("onesb", [128, 128], BF16)
        onespad = sb("onespad", [128, 192], BF16)
        maskf = sb("maskf", [128, 2, 128], F32)
        maskb = sb("maskb", [128, 2, 128], BF16)
        rstd = sb("rstd", [128, 512], F32)
        rsq = sb("rsq", [128, 512], F32)
        relu_t = sb("relu_t", [128, 2, 512], F32)
        sqst = sb("sqst", [128, 4, 512], BF16)
        carry = sb("carry", [128, NCH, 4], F32)
        psum = st.enter_context(nc.psum_tensor("psum", [128, 8, 512], F32))

        s_in = [P.new_dma_sem(f"s_in{i}") for i in range(2)]
        s_out = [P.new_dma_sem(f"s_out{i}") for i in range(2)]
        s_w = [P.new_dma_sem(f"s_w{i}") for i in range(NSLOT)]
        s_ws = [P.new_dma_sem(f"s_ws{i}") for i in range(NSLOT)]
        s_misc = P.new_dma_sem("s_misc")
        s_sh = P.new_dma_sem("s_sh")
        s_ck = [P.new_dma_sem(f"s_ck{i}") for i in range(2)]
        s_cv = [P.new_dma_sem(f"s_cv{i}") for i in range(2)]
        s_kvo = [P.new_dma_sem(f"s_kvo{i}") for i in range(2)]
        s_st = [P.new_dma_sem(f"s_st{i}") for i in range(2)]

        class UV:
            def __init__(self, off, dt, n):
                esz = 4 if dt == F32 else 2
                assert off % 4 == 0 and (n * esz) % 4 == 0
                assert off + n * esz <= 65536, (off, n, esz)
                a = U[:, off // 4:(off + n * esz) // 4]
                self.ap = a if dt == F32 else a.bitcast(BF16)
                self.keys = [("U", g) for g in range(off // 1024, (off + n * esz - 1) // 1024 + 1)]

        bank_ctr = [0]

        held_banks = set()

        def next_bank():
            while True:
                b = bank_ctr[0] % 8
                bank_ctr[0] += 1
                if b not in held_banks:
                    return b

        def vcol(v, c):
            return vecs[:, v * 16 + c:v * 16 + c + 1]

        def dcol(v, c):
            return dvec[:, v * 16 + c:v * 16 + c + 1]

        DV_HBA, DV_HBX, DV_CL, DV_CL2, DV_ESINK = range(5)

        P.dma("sp", vecs[:], vecs_d, s_misc, writes=["vecs"])
        s_wg = P.new_dma_sem("s_wg")
        P.dma_group("pool", [(wgate[:, 0, :, :], w_ga.rearrange("n c d -> c n d")),
                             (wgate[:, 1, :, :], w_gx.rearrange("n c d -> c n d"))], s_wg, writes=["wgate0", "wgate1"])

        P.op("pool", lambda e: e.memset(ident[:], 0.0), writes=["ident"])
        P.op("pool", lambda e: e.affine_select(out=ident[:], in_=ident[:], pattern=[[-1, 128]],
                                               compare_op=ALU.not_equal, fill=1.0, base=0, channel_multiplier=1),
             reads=["ident"], writes=["ident"])
        P.op("pool", lambda e: e.memset(maskf[:], 0.0), writes=["maskf"])
        P.op("pool", lambda e: e.affine_select(out=maskf[:, 0, :], in_=maskf[:, 0, :], pattern=[[-1, 128]],
                                               compare_op=ALU.is_gt, fill=NEG, base=0, channel_multiplier=1),
             reads=["maskf"], writes=["maskf"])
        P.op("pool", lambda e: e.affine_select(out=maskf[:, 1, :], in_=maskf[:, 1, :], pattern=[[1, 128]],
                                               compare_op=ALU.is_ge, fill=NEG, base=0, channel_multiplier=-1),
             reads=["maskf"], writes=["maskf"])
        P.op("dve", lambda e: e.tensor_copy(out=maskb[:], in_=maskf[:]), reads=["maskf"], writes=["maskb"])
        P.op("dve", lambda e: e.tensor_copy(out=identb[:], in_=ident[:]), reads=["ident"], writes=["identb"])
        P.op("dve", lambda e: e.memset(onesb[:], 1.0), writes=["onesb"])
        P.op("dve", lambda e: e.memset(onespad[:], 0.0), writes=["onespad"])
        P.op("dve", lambda e: e.memset(onespad[:, 64:128], 1.0), reads=["onespad"], writes=["onespad"])
        P.op("dve", lambda e: e.memset(Vpad[:], 0.0), writes=[("Vpad", i) for i in range(5)])
        P.op("dve", lambda e: e.memset(K2T[:], 0.0), writes=["K2T"])
        P.op("dve", lambda e: e.memset(carry[:], 0.0), writes=[("carry", c) for c in range(NCH)])
        P.op("dve", lambda e: e.tensor_scalar(out=dvec[:, 0:32], in0=vecs[:, V_BA * 16:V_BA * 16 + 32], scalar1=0.5,
                                              scalar2=None, op0=ALU.mult), reads=["vecs"], writes=["dv_hb"])
        P.op("act", lambda e: e.activation(out=dvec[:, 32:48], in_=vecs[:, V_LAM * 16:V_LAM * 16 + 16], func=AF.Exp,
                                           scale=-1.0), reads=["vecs"], writes=["dv_t"])
        P.op("act", lambda e: e.activation(out=dvec[:, 48:64], in_=dvec[:, 32:48], func=AF.Ln, bias=1.0),
             reads=["dv_t"], writes=["dv_sp"])
        P.op("dve", lambda e: e.tensor_scalar(out=dvec[:, 32:48], in0=dvec[:, 48:64], scalar1=-8.0, scalar2=None,
                                              op0=ALU.mult), reads=["dv_sp"], writes=["dv_t"])
        P.op("dve", lambda e: e.tensor_scalar(out=dvec[:, 48:64], in0=dvec[:, 32:48], scalar1=0.5, scalar2=None,
                                              op0=ALU.mult), reads=["dv_t"], writes=["dv_sp"])
        P.op("act", lambda e: e.activation(out=dvec[:, 64:80], in_=vecs[:, V_SINK * 16:V_SINK * 16 + 16], func=AF.Exp),
             reads=["vecs"], writes=["dv_es"])
        CONST_R = ["vecs", "dv_hb", "dv_t", "dv_sp", "dv_es"]

        blocks = []

        def add_block(parts, kc, nct):
            blocks.append((parts, kc, nct))

        for pi in passes:
            for cp in range(8):
                add_block([(w_in[:, cp * 256:(cp + 1) * 256], 0, 256), (w_in[:, D + cp * 256:D + (cp + 1) * 256], 256, 256)], 16, 512)
            for b in range(4):
                add_block([(w_out[:, b * 512:(b + 1) * 512], 0, 512)], 16, 512)
            for b in range(16):
                add_block([(w_up[0][:, b * 512:(b + 1) * 512], 0, 512)], 16, 512)
            for b in range(16):
                add_block([(w_dn[0][:, b * 128:(b + 1) * 128], 0, 128)], 64, 128)
            add_block([(w_kv[:, :], 0, 512)], 16, 512)
            for b in range(4):
                add_block([(w_q[:, b * 512:(b + 1) * 512], 0, 512)], 16, 512)
            for b in range(4):
                add_block([(w_o[:, b * 512:(b + 1) * 512], 0, 512)], 16, 512)
            for b in range(16):
                add_block([(w_up[1][:, b * 512:(b + 1) * 512], 0, 512)], 16, 512)
            for b in range(16):
                add_block([(w_dn[1][:, b * 128:(b + 1) * 128], 0, 128)], 64, 128)

        wstate = {"issued": 0, "next": 0}

        NBLK = 85

        NSTORE = 1

        def conv_pass(bi):
            if len(passes) < 3:
                return 0
            return 0 if bi % 3 == 0 else 1

        def stored(bi):
            return True

        def w_issue(i):
            parts, kc, nct = blocks[i]
            s = i % NSLOT
            pord, bi = divmod(i, NBLK)
            if pord > conv_pass(bi):
                P.dma("pool", wslot[s][:, :], wscr[bi], s_w[s], reads=[("wscr", bi)], writes=[("ws", s, j) for j in range(4)])
                return
            view = wslot[s][:, 0:kc * nct].rearrange("p (k n) -> p k n", k=kc)
            dmas = []
            for (src, off, n) in parts:
                srcv = src.rearrange("(k p) n -> p k n", p=128)
                for k0 in range(0, kc, 16):
                    dmas.append((view[:, k0:k0 + 16, off:off + n], srcv[:, k0:k0 + 16, :]))
            assert len(dmas) <= 4
            P.dma_group("pool", dmas, s_w[s], writes=[("ws", s, j) for j in range(4)])

        def w_acquire():
            i = wstate["next"]
            wstate["next"] += 1
            while wstate["issued"] < min(len(blocks), i + NSLOT):
                w_issue(wstate["issued"])
                wstate["issued"] += 1
            parts, kc, nct = blocks[i]
            s = i % NSLOT
            pord, bi = divmod(i, NBLK)
            if pord == conv_pass(bi) and pord < len(passes) - 1:
                P.dma("sp", wscr[bi], wslot[s][:, :], s_ws[s], reads=[("ws", s, j) for j in range(4)], writes=[("wscr", bi)])
            view = wslot[s][:, 0:kc * nct].rearrange("p (k n) -> p k n", k=kc)
            return view, [("ws", s, j) for j in range(4)]

        def mm_group(out_ap, bank, pairs, reads):
            n = len(pairs)

            def fn(e):
                ins = None
                for i, (l, r) in enumerate(pairs):
                    ins = e.matmul(out_ap, lhsT=l, rhs=r, start=(i == 0), stop=(i == n - 1))
                return ins
            P.op("pe", fn, reads=reads, writes=[("ps", bank)])

        stats = {}

        def stats_begin():
            bank = next_bank()
            held_banks.add(bank)
            stats["bank"] = bank

        def stats_sq(T, c, sq_ap, sq_keys):
            P.op("act", lambda e: e.activation(out=sq_ap, in_=hres[:, c, 0:T], func=AF.Square), reads=[("h", c)], writes=sq_keys)

        def stats_mm(T, c, sq_ap, sq_keys):
            bank = stats["bank"]
            P.op("pe", lambda e: e.matmul(psum[:, bank, 0:T], lhsT=onesb[:], rhs=sq_ap, start=(c == 0), stop=(c == NCH - 1)),
                 reads=["onesb"] + sq_keys, writes=[("ps", bank)])

        def stats_chunk(T, c, sq_ap, sq_keys):
            stats_sq(T, c, sq_ap, sq_keys)
            stats_mm(T, c, sq_ap, sq_keys)

        def stats_finish(T):
            bank = stats["bank"]
            P.op("act", lambda e: e.activation(out=rsq[:, 0:T], in_=psum[:, bank, 0:T], func=AF.Sqrt,
                                               scale=1.0 / D, bias=EPS), writes=[("ps", bank), "rsq"])
            P.op("dve", lambda e: e.reciprocal(out=rstd[:, 0:T], in_=rsq[:, 0:T]), reads=["rsq"], writes=["rstd"])
            held_banks.discard(bank)

        def stats_classic(T):
            stats_begin()
            for c in range(NCH):
                stats_chunk(T, c, xn[:, c, 0:T], [("xn", c)])
            stats_finish(T)

        def normalize(T, gv, dst_chunk, dst_keys):
            for c in range(NCH):
                P.op("dve", lambda e, c=c: e.scalar_tensor_tensor(out=dst_chunk(c), in0=hres[:, c, 0:T], scalar=vcol(gv, c),
                                                                  in1=rstd[:, 0:T], op0=ALU.mult, op1=ALU.mult),
                     reads=[("h", c), "rstd"] + CONST_R, writes=dst_keys(c))

        def scale_g(T, gv, c, eng):
            if eng == "act":
                P.op("act", lambda e: e.activation(out=xn[:, c, 0:T], in_=hres[:, c, 0:T], func=AF.Copy, scale=vcol(gv, c)),
                     reads=[("h", c)] + CONST_R, writes=[("xn", c)])
            else:
                P.op("dve", lambda e: e.tensor_scalar(out=xn[:, c, 0:T], in0=hres[:, c, 0:T], scalar1=vcol(gv, c), scalar2=None,
                                                      op0=ALU.mult), reads=[("h", c)] + CONST_R, writes=[("xn", c)])

        DEFER = 2

        def post_chunk(T, m, post):
            if post is None:
                return
            if m == 0:
                stats_begin()
            stats_sq(T, m, sqst[:, m % 4, 0:T], [("sqst", m % 4)])
            if post.get("xg") is not None:
                scale_g(T, post["xg"], m, "act")
            if m - DEFER >= 0:
                mm = m - DEFER
                stats_mm(T, mm, sqst[:, mm % 4, 0:T], [("sqst", mm % 4)])
            if m == NCH - 1:
                for mm in range(NCH - DEFER, NCH):
                    stats_mm(T, mm, sqst[:, mm % 4, 0:T], [("sqst", mm % 4)])
                stats_finish(T)

        def gemm_to_h(T, rhs_chunk, rhs_keys, post=None):
            for b in range(4):
                view, wk = w_acquire()
                for j in range(4):
                    m = b * 4 + j
                    bank = next_bank()
                    mm_group(psum[:, bank, 0:T], bank, [(view[:, k, j * 128:(j + 1) * 128], rhs_chunk(k)) for k in range(NCH)],
                             reads=wk + [kk for k in range(NCH) for kk in rhs_keys(k)])
                    P.op("dve", lambda e, m=m, bank=bank: e.tensor_tensor(out=hres[:, m, 0:T], in0=hres[:, m, 0:T],
                                                                         in1=psum[:, bank, 0:T], op=ALU.add),
                         reads=[("h", m)], writes=[("ps", bank), ("h", m)])
                    post_chunk(T, m, post)

        def mlp(T, post):
            hid = [UV(j * T * 2, BF16, T) for j in range(64)]
            for b in range(16):
                view, wk = w_acquire()
                for j in range(4):
                    m = b * 4 + j
                    bank = next_bank()
                    mm_group(psum[:, bank, 0:T], bank, [(view[:, k, j * 128:(j + 1) * 128], xn[:, k, 0:T]) for k in range(NCH)],
                             reads=wk + [("xn", k) for k in range(NCH)])
                    r = m % 2
                    P.op("dve", lambda e, r=r, bank=bank: e.scalar_tensor_tensor(out=relu_t[:, r, 0:T], in0=psum[:, bank, 0:T], scalar=0.0,
                                                                                in1=rstd[:, 0:T], op0=ALU.max, op1=ALU.mult),
                         reads=["rstd"], writes=[("ps", bank), ("relu", r)])
                    P.op("act", lambda e, r=r, m=m: e.activation(out=hid[m].ap, in_=relu_t[:, r, 0:T], func=AF.Square),
                         reads=[("relu", r)], writes=hid[m].keys)
            for m in range(16):
                view, wk = w_acquire()
                bank = next_bank()
                mm_group(psum[:, bank, 0:T], bank, [(view[:, k, :], hid[k].ap) for k in range(64)],
                         reads=wk + [kk for k in range(64) for kk in hid[k].keys])
                P.op("dve", lambda e, m=m, bank=bank: e.tensor_tensor(out=hres[:, m, 0:T], in0=hres[:, m, 0:T],
                                                                     in1=psum[:, bank, 0:T], op=ALU.add),
                     reads=[("h", m)], writes=[("ps", bank), ("h", m)])
                post_chunk(T, m, post)

        def transposes_out(src_chunk, src_keys, ncols, stage, stage_keys):
            for cg in range(4):
                bank = next_bank()

                def fn(e, cg=cg, bank=bank):
                    ins = None
                    for j in range(4):
                        ins = e.transpose(out=psum[0:ncols, bank, j * 128:(j + 1) * 128], in_=src_chunk(cg * 4 + j), identity=ident[:])
                    return ins
                P.op("pe", fn, reads=["ident"] + [kk for j in range(4) for kk in src_keys(cg * 4 + j)], writes=[("ps", bank)])
                eng = "act" if cg % 2 else "dve"
                if eng == "act":
                    P.op("act", lambda e, cg=cg, bank=bank: e.activation(out=stage[0:ncols, cg * 512:(cg + 1) * 512],
                                                                        in_=psum[0:ncols, bank, :], func=AF.Copy),
                         writes=[("ps", bank)] + stage_keys)
                else:
                    P.op("dve", lambda e, cg=cg, bank=bank: e.tensor_copy(out=stage[0:ncols, cg * 512:(cg + 1) * 512],
                                                                         in_=psum[0:ncols, bank, :]),
                         writes=[("ps", bank)] + stage_keys)

        def run_pass(pi):
            is_s = (pi == 4)
            T = 64 if is_s else 512
            last_p = (pi == 3)
            io_stage = [UV(i * 8192, F32, 2048) for i in range(2)]

            def tv(ap):
                return ap.rearrange("p (b t) -> p b t", t=4) if is_s else ap

            if not is_s:
                for tb in range(4):
                    sg = io_stage[tb % 2]
                    P.dma("sp", sg.ap, xp[pi * 512 + tb * 128:pi * 512 + (tb + 1) * 128, :], s_in[tb % 2], writes=sg.keys)
                    for cg in range(4):
                        bank = next_bank()

                        def fn(e, cg=cg, bank=bank, sg=sg):
                            ins = None
                            for j in range(4):
                                c = cg * 4 + j
                                ins = e.transpose(out=psum[:, bank, j * 128:(j + 1) * 128], in_=sg.ap[:, c * 128:(c + 1) * 128],
                                                  identity=ident[:])
                            return ins
                        P.op("pe", fn, reads=["ident"] + sg.keys, writes=[("ps", bank)])
                        wr = [("h", cg * 4 + j) for j in range(4)]
                        if cg % 2:
                            P.op("act", lambda e, cg=cg, bank=bank, tb=tb: e.activation(
                                out=hres[:, cg * 4:(cg + 1) * 4, tb * 128:(tb + 1) * 128],
                                in_=psum[:, bank, :].rearrange("p (j t) -> p j t", j=4), func=AF.Copy), writes=[("ps", bank)] + wr)
                        else:
                            P.op("dve", lambda e, cg=cg, bank=bank, tb=tb: e.tensor_copy(
                                out=hres[:, cg * 4:(cg + 1) * 4, tb * 128:(tb + 1) * 128],
                                in_=psum[:, bank, :].rearrange("p (j t) -> p j t", j=4)), writes=[("ps", bank)] + wr)
                    cols = slice(tb * 128, (tb + 1) * 128)
                    if tb == 0:
                        stats_begin()
                    sbank = stats["bank"]
                    for cg in range(4):
                        P.op("act", lambda e, cg=cg, cols=cols: e.activation(out=xn[:, cg * 4:(cg + 1) * 4, cols],
                                                                            in_=hres[:, cg * 4:(cg + 1) * 4, cols], func=AF.Square),
                             reads=[("h", cg * 4 + j) for j in range(4)], writes=[("xn", cg * 4 + j) for j in range(4)])

                    def fn_s(e, cols=cols, sbank=sbank):
                        ins = None
                        for c in range(NCH):
                            ins = e.matmul(psum[:, sbank, cols], lhsT=onesb[:], rhs=xn[:, c, cols], start=(c == 0), stop=(c == NCH - 1),
                                           skip_group_check=True)
                        return ins
                    P.op("pe", fn_s, reads=["onesb"] + [("xn", c) for c in range(NCH)], writes=[("ps", sbank)])
                    P.op("act", lambda e, cols=cols, sbank=sbank: e.activation(out=rsq[:, cols], in_=psum[:, sbank, cols], func=AF.Sqrt,
                                                                              scale=1.0 / D, bias=EPS), writes=[("ps", sbank), "rsq"])
                    P.op("dve", lambda e, cols=cols: e.reciprocal(out=rstd[:, cols], in_=rsq[:, cols]), reads=["rsq"], writes=["rstd"])
                    for c in range(NCH):
                        P.op("dve", lambda e, c=c, cols=cols: e.scalar_tensor_tensor(out=xn[:, c, cols], in0=hres[:, c, cols],
                                                                                    scalar=vcol(V_GMIX0, c), in1=rstd[:, cols],
                                                                                    op0=ALU.mult, op1=ALU.mult),
                             reads=[("h", c), "rstd"] + CONST_R, writes=[("xn", c)])
                    if tb == 3:
                        held_banks.discard(sbank)
            else:
                sg = io_stage[0]
                P.dma("sp", sg.ap[0:64, :], xs, s_in[0], writes=sg.keys)
                for cg in range(4):
                    bank = next_bank()

                    def fn(e, cg=cg, bank=bank, sg=sg):
                        ins = None
                        for j in range(4):
                            c = cg * 4 + j
                            ins = e.transpose(out=psum[:, bank, j * 64:(j + 1) * 64], in_=sg.ap[0:64, c * 128:(c + 1) * 128],
                                              identity=ident[0:64, 0:64])
                        return ins
                    P.op("pe", fn, reads=["ident"] + sg.keys, writes=[("ps", bank)])
                    P.op("dve", lambda e, cg=cg, bank=bank: e.tensor_copy(
                        out=hres[:, cg * 4:(cg + 1) * 4, 0:64], in_=psum[:, bank, 0:256].rearrange("p (j t) -> p j t", j=4)),
                        writes=[("ps", bank)] + [("h", cg * 4 + j) for j in range(4)])
                sconvT = UV(48 * 1024, F32, 16 * 48)
                h0s = UV(51 * 1024, F32, 16 * 16)
                sg1 = io_stage[1]
                P.dma("sp", sg1.ap[0:48, :], sconv, s_in[1], writes=sg1.keys)
                for cg in range(4):
                    bank = next_bank()

                    def fn(e, cg=cg, bank=bank):
                        ins = None
                        for j in range(4):
                            c = cg * 4 + j
                            ins = e.transpose(out=psum[:, bank, j * 48:(j + 1) * 48], in_=sg1.ap[0:48, c * 128:(c + 1) * 128],
                                              identity=ident[0:48, 0:48])
                        return ins
                    P.op("pe", fn, reads=["ident"] + sg1.keys, writes=[("ps", bank)])
                    P.op("dve", lambda e, cg=cg, bank=bank: e.tensor_copy(out=sconvT.ap[:, cg * 192:(cg + 1) * 192],
                                                                         in_=psum[:, bank, 0:192]),
                         writes=[("ps", bank)] + sconvT.keys)
                sg2 = UV(32 * 1024, F32, 2048)
                P.dma("sp", sg2.ap[0:16, :], sh, s_sh, writes=sg2.keys)
                for cg in range(4):
                    bank = next_bank()

                    def fn(e, cg=cg, bank=bank):
                        ins = None
                        for j in range(4):
                            c = cg * 4 + j
                            ins = e.transpose(out=psum[:, bank, j * 16:(j + 1) * 16], in_=sg2.ap[0:16, c * 128:(c + 1) * 128],
                                              identity=ident[0:16, 0:16])
                        return ins
                    P.op("pe", fn, reads=["ident"] + sg2.keys, writes=[("ps", bank)])
                    P.op("dve", lambda e, cg=cg, bank=bank: e.tensor_copy(out=h0s.ap[:, cg * 64:(cg + 1) * 64],
                                                                         in_=psum[:, bank, 0:64]),
                         writes=[("ps", bank)] + h0s.keys)

            if phase_limit < 1:
                return
            if is_s:
                stats_classic(T)
                normalize(T, V_GMIX0, lambda c: xn[:, c, 0:T], lambda c: [("xn", c)])
            yoff = 52 * 1024 if is_s else 0
            y_in = [UV(yoff + c * T * 2, BF16, T) for c in range(NCH)]
            TW = 512 if is_s else 2048
            tbase = 16 * 1024
            SLOT = {"gg": (0, 4), "xc": (4, 3), "thi": (7, 3), "a": (10, 3), "sq": (13, 3), "thr": (16, 2), "u": (18, 1),
                    "hs": (19, 1), "xcb": (20, 1)}

            def tmp(name, c, n=None, dt=F32):
                base, nb = SLOT[name]
                if name == "xcb":
                    return UV(tbase + base * TW + (c % 2) * (TW // 2), BF16, T)
                return UV(tbase + (base + c % nb) * TW, dt, n if n is not None else T)

            def xre_buf(c):
                return UV((58 * 1024 if not is_s else 56 * 1024) + (c % 2) * 2304, F32, NXE)
            if is_s:
                xr_all = [UV(40 * 1024 + c * 256, F32, 64) for c in range(NCH)]
                hs_all = [UV(44 * 1024 + c * 256, F32, 64) for c in range(NCH)]
            NXE = 112 if is_s else 515

            rec_w = {}

            def st_A_pe(c):
                if c % 2 == 0:
                    rec_w["view"], rec_w["wk"] = w_acquire()
                view, wk = rec_w["view"], rec_w["wk"]
                jj = c % 2
                b1 = next_bank()
                mm_group(psum[:, b1, 0:T], b1, [(view[:, k, jj * 128:(jj + 1) * 128], xn[:, k, 0:T]) for k in range(NCH)],
                         reads=wk + [("xn", k) for k in range(NCH)])
                b2 = next_bank()
                mm_group(psum[:, b2, 0:T], b2, [(view[:, k, 256 + jj * 128:256 + (jj + 1) * 128], xn[:, k, 0:T]) for k in range(NCH)],
                         reads=wk + [("xn", k) for k in range(NCH)])
                rec_w[("banks", c)] = (b1, b2)

            def st_A_evac(c):
                b1, b2 = rec_w[("banks", c)]
                gg = tmp("gg", c)
                xre = xre_buf(c)
                P.op("act", lambda e: e.activation(out=gg.ap, in_=psum[:, b1, 0:T], func=AF.Gelu_apprx_tanh),
                     writes=[("ps", b1)] + gg.keys)
                if is_s:
                    xv = xre.ap.rearrange("p (b s) -> p b s", s=7)
                    P.op("dve", lambda e: e.tensor_copy(out=xv[:, :, 3:7], in_=psum[:, b2, 0:64].rearrange("p (b t) -> p b t", t=4)),
                         writes=[("ps", b2)] + xre.keys)
                    P.op("dve", lambda e: e.tensor_copy(out=xv[:, :, 0:3],
                                                        in_=sconvT.ap[:, c * 48:(c + 1) * 48].rearrange("p (b j) -> p b j", j=3)),
                         reads=sconvT.keys, writes=xre.keys)
                    P.op("act", lambda e: e.activation(out=xr_all[c].ap, in_=psum[:, b2, 0:64], func=AF.Copy),
                         writes=[("ps", b2)] + xr_all[c].keys)
                else:
                    P.op("dve", lambda e: e.tensor_copy(out=xre.ap[:, 3:515], in_=psum[:, b2, 0:512]),
                         writes=[("ps", b2)] + xre.keys)
                    P.op("dve", lambda e: e.tensor_copy(out=xre.ap[:, 0:3], in_=carry[:, c, 0:3]),
                         reads=[("carry", c)], writes=xre.keys)

            def st_B1_dve(c):
                xre = xre_buf(c)
                xc = tmp("xc", c)
                xcb = tmp("xcb", c)
                if is_s:
                    xv = xre.ap.rearrange("p (b s) -> p b s", s=7)
                    sh_ = lambda j: xv[:, :, j:j + 4]
                else:
                    sh_ = lambda j: xre.ap[:, j:j + 512]
                P.op("dve", lambda e: e.tensor_scalar(out=tv(xc.ap), in0=sh_(0), scalar1=vcol(V_CW0, c), scalar2=vcol(V_CB, c),
                                                      op0=ALU.mult, op1=ALU.add), reads=xre.keys + CONST_R, writes=xc.keys)
                for j in range(1, 4):
                    P.op("dve", lambda e, j=j: e.scalar_tensor_tensor(out=tv(xc.ap), in0=sh_(j), scalar=vcol(V_CW0 + j, c), in1=tv(xc.ap),
                                                                      op0=ALU.mult, op1=ALU.add),
                         reads=xre.keys + xc.keys + CONST_R, writes=xc.keys)
                if not is_s:
                    P.op("dve", lambda e: e.tensor_copy(out=carry[:, c, 0:3], in_=xre.ap[:, 512:515]), reads=xre.keys,
                         writes=[("carry", c)])
                P.op("dve", lambda e: e.tensor_copy(out=xcb.ap, in_=xc.ap), reads=xc.keys, writes=xcb.keys)

            def st_B1_pe(c):
                xcb = tmp("xcb", c)
                b1 = next_bank()
                mm_group(psum[:, b1, 0:T], b1, [(wgate[:, 0, c, :], xcb.ap)], reads=["wgate0"] + xcb.keys)
                b2 = next_bank()
                mm_group(psum[:, b2, 0:T], b2, [(wgate[:, 1, c, :], xcb.ap)], reads=["wgate1"] + xcb.keys)
                rec_w[("gbanks", c)] = (b1, b2)

            def st_B1_act(c):
                b1, b2 = rec_w[("gbanks", c)]
                thr, thi, a, sq = tmp("thr", c), tmp("thi", c), tmp("a", c), tmp("sq", c)
                P.op("act", lambda e: e.activation(out=thr.ap, in_=psum[:, b1, 0:T], func=AF.Tanh, scale=0.5, bias=dcol(DV_HBA, c)),
                     reads=CONST_R, writes=[("ps", b1)] + thr.keys)
                P.op("act", lambda e: e.activation(out=thi.ap, in_=psum[:, b2, 0:T], func=AF.Tanh, scale=0.5, bias=dcol(DV_HBX, c)),
                     reads=CONST_R, writes=[("ps", b2)] + thi.keys)
                P.op("act", lambda e: e.activation(out=a.ap, in_=thr.ap, func=AF.Exp, scale=dcol(DV_CL2, c), bias=dcol(DV_CL2, c)),
                     reads=thr.keys + CONST_R, writes=a.keys)
                P.op("act", lambda e: e.activation(out=sq.ap, in_=thr.ap, func=AF.Exp, scale=dcol(DV_CL, c), bias=dcol(DV_CL, c)),
                     reads=thr.keys + CONST_R, writes=sq.keys)
                P.op("act", lambda e: e.activation(out=sq.ap, in_=sq.ap, func=AF.Sqrt, scale=-0.25, bias=0.25),
                     reads=sq.keys, writes=sq.keys)

            def st_B2(c):
                gg, xc, thi, a, sq = tmp("gg", c), tmp("xc", c), tmp("thi", c), tmp("a", c), tmp("sq", c)
                u, hs = tmp("u", c), tmp("hs", c)
                P.op("dve", lambda e: e.scalar_tensor_tensor(out=u.ap, in0=thi.ap, scalar=1.0, in1=xc.ap, op0=ALU.add, op1=ALU.mult),
                     reads=thi.keys + xc.keys, writes=u.keys)
                P.op("dve", lambda e: e.tensor_tensor(out=u.ap, in0=u.ap, in1=sq.ap, op=ALU.mult), reads=u.keys + sq.keys, writes=u.keys)
                if not is_s:
                    P.op("dve", lambda e: e.tensor_tensor_scan(out=hs.ap, data0=a.ap, data1=u.ap, initial=carry[:, c, 3:4],
                                                               op0=ALU.mult, op1=ALU.add),
                         reads=a.keys + u.keys + [("carry", c)], writes=hs.keys)
                    P.op("dve", lambda e: e.tensor_copy(out=carry[:, c, 3:4], in_=hs.ap[:, 511:512]), reads=hs.keys,
                         writes=[("carry", c)])
                else:
                    av, uv, hv = tv(a.ap), tv(u.ap), tv(hs.ap)
                    for t in range(4):
                        prev = h0s.ap[:, c * 16:(c + 1) * 16] if t == 0 else hv[:, :, t - 1]
                        P.op("dve", lambda e, t=t, prev=prev: e.tensor_tensor(out=hv[:, :, t], in0=av[:, :, t], in1=prev, op=ALU.mult),
                             reads=a.keys + hs.keys + h0s.keys, writes=hs.keys)
                        P.op("dve", lambda e, t=t: e.tensor_tensor(out=hv[:, :, t], in0=hv[:, :, t], in1=uv[:, :, t], op=ALU.add),
                             reads=u.keys + hs.keys, writes=hs.keys)
                    P.op("act", lambda e: e.activation(out=hs_all[c].ap, in_=hs.ap, func=AF.Copy), reads=hs.keys, writes=hs_all[c].keys)
                P.op("dve", lambda e: e.tensor_tensor(out=y_in[c].ap, in0=gg.ap, in1=hs.ap, op=ALU.mult),
                     reads=gg.keys + hs.keys, writes=y_in[c].keys)

            for i in range(NCH + 3):
                if i < NCH:
                    st_A_pe(i)
                if 0 <= i - 3 < NCH:
                    st_B2(i - 3)
                if 0 <= i - 1 < NCH:
                    st_B1_dve(i - 1)
                    st_B1_pe(i - 1)
                if i < NCH:
                    st_A_evac(i)
                if 0 <= i - 1 < NCH:
                    st_B1_act(i - 1)

            if phase_limit < 2:
                return
            gemm_to_h(T, lambda k: y_in[k].ap, lambda k: y_in[k].keys, post={"xg": V_GMLP0})
            if last_p:
                stg = io_stage[0]
                transposes_out(lambda c: carry[:, c, :], lambda c: [("carry", c)], 4, stg.ap, stg.keys)
                P.dma_group("sp", [(o_convp, stg.ap[0:3, :]), (o_hp, stg.ap[3:4, :])], s_st[0], reads=stg.keys)
            if is_s:
                stg = io_stage[0]
                transposes_out(lambda c: xr_all[c].ap, lambda c: xr_all[c].keys, 64, stg.ap, stg.keys)
                ps_ = stg.ap.ap[0][0]
                P.dma_group("sp", [(o_convs[:, j, :], bass.AP(stg.ap.tensor, stg.ap.offset + (1 + j) * ps_, [[4 * ps_, 16], [1, 2048]]))
                                   for j in range(3)], s_st[0], reads=stg.keys)
                stg1 = io_stage[1]
                transposes_out(lambda c: hs_all[c].ap, lambda c: hs_all[c].keys, 64, stg1.ap, stg1.keys)
                ps1 = stg1.ap.ap[0][0]
                src = bass.AP(stg1.ap.tensor, stg1.ap.offset + 3 * ps1, [[4 * ps1, 16], [1, 2048]])
                P.dma("sp", o_hs, src, s_st[1], reads=stg1.keys)

            if phase_limit < 3:
                return
            mlp(T, post={"xg": V_GKV})

            if phase_limit < 4:
                return
            rtm = UV(60 * 1024, F32, 16)
            bank = next_bank()
            ntr = 16 if is_s else 4
            wtr = 4 if is_s else 128

            def fn_r(e, bank=bank):
                ins = None
                for i in range(ntr):
                    ins = e.transpose(out=psum[0:wtr, bank, i:i + 1], in_=rstd[0:1, i * wtr:(i + 1) * wtr], identity=ident[0:1, 0:1])
                return ins
            P.op("pe", fn_r, reads=["ident", "rstd"], writes=[("ps", bank)])
            P.op("dve", lambda e, bank=bank: e.tensor_copy(out=rtm.ap[0:wtr, 0:ntr], in_=psum[0:wtr, bank, 0:ntr]),
                 writes=[("ps", bank)] + rtm.keys)
            view, wk = w_acquire()
            koff = 0 if is_s else 128
            for g in range(4):
                bank = next_bank()
                n = NCH

                def fn(e, g=g, bank=bank, view=view):
                    ins = None
                    for half in range(2):
                        for k in range(n):
                            ins = e.matmul(psum[half * 64:(half + 1) * 64, bank, 0:T], lhsT=view[:, k, g * 64:(g + 1) * 64],
                                           rhs=xn[:, k, 0:T], start=(k == 0), stop=(k == n - 1))
                    return ins
                P.op("pe", fn, reads=wk + [("xn", k) for k in range(NCH)], writes=[("ps", bank)])
                for half in range(2):
                    P.op("dve", lambda e, g=g, bank=bank, half=half: e.tensor_tensor(
                        out=K2T[half * 64:(half + 1) * 64, g, half, koff:koff + T], in0=psum[half * 64:(half + 1) * 64, bank, 0:T],
                        in1=rstd[half * 64:(half + 1) * 64, 0:T], op=ALU.mult), reads=["rstd"], writes=[("ps", bank), "K2T"])
            if not is_s:
                for tb in range(4):
                    bank = next_bank()
                    mm_group(psum[:, bank, :], bank, [(xn[:, k, tb * 128:(tb + 1) * 128], view[:, k, :]) for k in range(NCH)],
                             reads=wk + [("xn", k) for k in range(NCH)])
                    P.op("dve", lambda e, tb=tb, bank=bank: e.tensor_scalar(
                        out=Vpad[:, tb + 1, :, 64:128], in0=psum[:, bank, 256:512].rearrange("p (g d) -> p g d", g=4),
                        scalar1=rtm.ap[:, tb:tb + 1], scalar2=None, op0=ALU.mult),
                        reads=rtm.keys, writes=[("ps", bank), ("Vpad", tb + 1)])
                    if last_p and tb == 3:
                        kvo = UV(40 * 1024, F32, 512)
                        P.op("act", lambda e, bank=bank, tb=tb: e.activation(out=kvo.ap, in_=psum[:, bank, :], func=AF.Copy,
                                                                            scale=rtm.ap[:, tb:tb + 1]),
                             reads=rtm.keys, writes=[("ps", bank)] + kvo.keys)
                        P.dma_group("sp", [(o_kp, kvo.ap[:, 0:256]), (o_vp, kvo.ap[:, 256:512])], s_kvo[0], reads=kvo.keys)
            else:
                vpn = [UV(40 * 1024 + i * 1024, BF16, 512) for i in range(16)]
                kvos = [UV(24 * 1024 + i * 2048, F32, 512) for i in range(2)]
                for bb in range(16):
                    bank = next_bank()
                    mm_group(psum[0:4, bank, :], bank, [(xn[:, k, bb * 4:(bb + 1) * 4], view[:, k, :]) for k in range(NCH)],
                             reads=wk + [("xn", k) for k in range(NCH)])
                    pv = psum[0:4, bank, 256:512]
                    vsrc = bass.AP(pv.tensor, pv.offset, [list(pv.ap[0]), [64, 4], [0, 2], [1, 64]])
                    P.op("dve", lambda e, bb=bb, vsrc=vsrc: e.tensor_scalar(
                        out=vpn[bb].ap[0:4, :].rearrange("p (g r d) -> p g r d", g=4, r=2), in0=vsrc,
                        scalar1=rtm.ap[0:4, bb:bb + 1], scalar2=None, op0=ALU.mult),
                        reads=rtm.keys, writes=[("ps", bank)] + vpn[bb].keys)
                    kvo = kvos[bb % 2]
                    P.op("act", lambda e, bank=bank, kvo=kvo, bb=bb: e.activation(out=kvo.ap[0:4, :], in_=psum[0:4, bank, :], func=AF.Copy,
                                                                                scale=rtm.ap[0:4, bb:bb + 1]),
                         reads=rtm.keys, writes=[("ps", bank)] + kvo.keys)
                    P.dma_group("sp", [(o_ks[bb * 4:(bb + 1) * 4, :], kvo.ap[0:4, 0:256]),
                                       (o_vs[bb * 4:(bb + 1) * 4, :], kvo.ap[0:4, 256:512])], s_kvo[bb % 2], reads=kvo.keys)

            if phase_limit < 5:
                return
            for c in range(NCH):
                scale_g(T, V_GMIX1, c, "act" if c % 2 else "dve")
            qT = [UV(c * T * 2, BF16, T) for c in range(NCH)]
            oT = [UV(NCH * T * 2 + c * T * 2, BF16, T) for c in range(NCH)]
            for b in range(4):
                view, wk = w_acquire()
                for j in range(4):
                    m = b * 4 + j
                    bank = next_bank()
                    mm_group(psum[:, bank, 0:T], bank, [(view[:, k, j * 128:(j + 1) * 128], xn[:, k, 0:T]) for k in range(NCH)],
                             reads=wk + [("xn", k) for k in range(NCH)])
                    P.op("dve", lambda e, m=m, bank=bank: e.tensor_tensor(out=qT[m].ap, in0=psum[:, bank, 0:T], in1=rstd[:, 0:T], op=ALU.mult),
                         reads=["rstd"], writes=[("ps", bank)] + qT[m].keys)

            Pt = [[UV(32 * 1024 + (s * 2 + kb) * 1024, BF16, 512) for kb in range(2)] for s in range(2)]
            den = [UV(36 * 1024 + s * 1024, F32, 256) for s in range(2)]
            unit_ctr = [0]

            def attn_unit(chunks, nq, qsl, kblocks):
                s = unit_ctr[0] % 2
                unit_ctr[0] += 1
                ncol = 2 * len(chunks) * nq
                wo = len(chunks) * nq
                for kb, (nk, kfn, vfn, mi, rk) in enumerate(kblocks):
                    bank = next_bank()

                    def fn(e, bank=bank, nk=nk, kfn=kfn, mi=mi):
                        mb = maskb[0:nk, mi, 0:nq]
                        mrhs = bass.AP(mb.tensor, mb.offset, [list(mb.ap[0]), [0, 2 * len(chunks)], [1, nq]])
                        ins = e.matmul(psum[0:nk, bank, 0:ncol], lhsT=identb[0:nk, 0:nk], rhs=mrhs, start=True, stop=False, skip_group_check=True)
                        for ci, c in enumerate(chunks):
                            g = c // 4
                            for half in range(2):
                                j = 2 * ci + half
                                ins = e.matmul(psum[0:nk, bank, j * nq:(j + 1) * nq], lhsT=kfn(g, half),
                                               rhs=qT[c].ap[:, qsl], start=False,
                                               stop=(ci == len(chunks) - 1 and half == 1), skip_group_check=True)
                        return ins
                    P.op("pe", fn, reads=["identb", "maskb"] + rk + [kk for c in chunks for kk in qT[c].keys], writes=[("ps", bank)])
                    pt = Pt[s][kb]
                    P.op("act", lambda e, bank=bank, nk=nk, pt=pt: e.activation(out=pt.ap[0:nk, 0:ncol], in_=psum[0:nk, bank, 0:ncol],
                                                                              func=AF.Exp, scale=0.125),
                         writes=[("ps", bank)] + pt.keys)
                bank = next_bank()

                def fn2(e, bank=bank):
                    ins = None
                    for region, off in ((0, 0), (1, 256)):
                        for ci, c in enumerate(chunks):
                            g = c // 4
                            first = True
                            nmm = 2 * len(kblocks)
                            cnt = 0
                            for kb, (nk, kfn, vfn, mi, rk) in enumerate(kblocks):
                                for half in range(2):
                                    j = 2 * ci + half
                                    if region == 0:
                                        vp = vfn(g)
                                    else:
                                        vp = onespad[0:nk, :]
                                    lhsT = vp[:, 64:192] if half == 0 else vp[:, 0:128]
                                    cnt += 1
                                    ins = e.matmul(psum[:, bank, off + ci * nq:off + (ci + 1) * nq], lhsT=lhsT,
                                                   rhs=Pt[s][kb].ap[0:nk, j * nq:(j + 1) * nq], start=first, stop=(cnt == nmm),
                                                   skip_group_check=True)
                                    first = False
                    return ins
                P.op("pe", fn2, reads=["onespad"] + [kk for kb in range(len(kblocks)) for kk in Pt[s][kb].keys] +
                     [kk for (_, _, _, _, rk) in kblocks for kk in rk], writes=[("ps", bank)])
                c0 = chunks[0]
                nch = len(chunks)
                dn = den[s]
                es = dvec[:, DV_ESINK * 16 + c0:DV_ESINK * 16 + c0 + nch]
                esb = bass.AP(es.tensor, es.offset, [list(es.ap[0]), [1, nch], [0, nq]])
                P.op("dve", lambda e, bank=bank: e.tensor_tensor(out=dn.ap[:, 0:wo].rearrange("p (c q) -> p c q", c=nch),
                                                                 in0=psum[:, bank, 256:256 + wo].rearrange("p (c q) -> p c q", c=nch),
                                                                 in1=esb, op=ALU.add),
                     reads=CONST_R, writes=[("ps", bank)] + dn.keys)
                P.op("dve", lambda e: e.reciprocal(out=dn.ap[:, 0:wo], in_=dn.ap[:, 0:wo]), reads=dn.keys, writes=dn.keys)
                for ci, c in enumerate(chunks):
                    P.op("dve", lambda e, ci=ci, c=c, bank=bank: e.tensor_tensor(out=oT[c].ap[:, qsl], in0=psum[:, bank, ci * nq:(ci + 1) * nq],
                                                                                in1=dn.ap[:, ci * nq:(ci + 1) * nq], op=ALU.mult),
                         reads=dn.keys, writes=[("ps", bank)] + oT[c].keys)

            if not is_s:
                for qb in range(4):
                    qsl = slice(qb * 128, (qb + 1) * 128)
                    kbl = []
                    if not (pi == 0 and qb == 0):
                        kbl.append((128, (lambda g, half, qb=qb: K2T[:, g, half, qb * 128:(qb + 1) * 128]),
                                    (lambda g, qb=qb: Vpad[:, qb, g, :]), 0, ["K2T", ("Vpad", qb)]))
                    kbl.append((128, (lambda g, half, qb=qb: K2T[:, g, half, (qb + 1) * 128:(qb + 2) * 128]),
                                (lambda g, qb=qb: Vpad[:, qb + 1, g, :]), 1, ["K2T", ("Vpad", qb + 1)]))
                    for g in range(4):
                        for cp in range(2):
                            attn_unit([4 * g + 2 * cp, 4 * g + 2 * cp + 1], 128, qsl, kbl)
                K2v = K2T[:].rearrange("p g h t -> p (g h) t")
                P.op("dve", lambda e: e.tensor_copy(out=K2v[:, :, 0:128], in_=K2v[:, :, 512:640]), reads=["K2T"], writes=["K2T"])
                P.op("dve", lambda e: e.tensor_copy(out=Vpad[:, 0, :, :], in_=Vpad[:, 4, :, :]), reads=[("Vpad", 4)], writes=[("Vpad", 0)])
            else:
                ckst = [UV(4 * 1024 + i * 2048, F32, 512) for i in range(2)]
                vcst = [UV(8 * 1024 + i * 1024, F32, 256) for i in range(2)]
                k2c = [UV(10 * 1024 + i * 2048, BF16, 1024) for i in range(2)]
                vvc = [UV(14 * 1024 + i * 1024, BF16, 512) for i in range(2)]
                qT_all = UV(0, BF16, NCH * T).ap.rearrange("p (c t) -> p c t", c=NCH)
                oT_all = UV(NCH * T * 2, BF16, NCH * T).ap.rearrange("p (c t) -> p c t", c=NCH)
                q_keys = [kk for c in range(NCH) for kk in qT[c].keys]
                o_keys = [kk for c in range(NCH) for kk in oT[c].keys]
                for i in range(2):
                    P.op("dve", lambda e, i=i: e.memset(k2c[i].ap, 0.0), writes=k2c[i].keys)
                esk = dvec[:, DV_ESINK * 16:DV_ESINK * 16 + 16]
                for bb in range(16):
                    s2 = bb % 2
                    ckv = ckst[s2].ap.rearrange("p (g r d) -> p g r d", g=4, r=2)
                    src = ck[bb].rearrange("t (g d) -> t g d", g=4)
                    P.dma_group("sp", [(ckv[:, :, 0, :], src), (ckv[:, :, 1, :], src)], s_ck[s2],
                                writes=ckst[s2].keys + [("ckdup", s2)])
                    P.dma("sp", vcst[s2].ap, cv[bb], s_cv[s2], writes=vcst[s2].keys)
                    bank = next_bank()

                    def fn(e, bank=bank, s2=s2):
                        ins = None
                        for g in range(4):
                            ins = e.transpose(out=psum[:, bank, g * 128:(g + 1) * 128], in_=ckst[s2].ap[:, g * 128:(g + 1) * 128],
                                              identity=ident[:])
                        return ins
                    P.op("pe", fn, reads=["ident", ("ckdup", s2)] + ckst[s2].keys, writes=[("ps", bank)])
                    for half in range(2):
                        P.op("act", lambda e, bank=bank, s2=s2, half=half: e.activation(
                            out=k2c[s2].ap.rearrange("p (g h t) -> p g h t", g=4, h=2)[half * 64:(half + 1) * 64, :, half, :],
                            in_=psum[half * 64:(half + 1) * 64, bank, :].rearrange("p (g t) -> p g t", g=4), func=AF.Copy),
                            reads=k2c[s2].keys, writes=[("ps", bank)] + k2c[s2].keys)
                    vc = vcst[s2].ap
                    vcsrc = bass.AP(vc.tensor, vc.offset, [list(vc.ap[0]), [64, 4], [0, 2], [1, 64]])
                    P.op("dve", lambda e, s2=s2, vcsrc=vcsrc: e.tensor_copy(
                        out=vvc[s2].ap.rearrange("p (g r d) -> p g r d", g=4, r=2), in_=vcsrc),
                        reads=vcst[s2].keys, writes=vvc[s2].keys)
                    u_s = unit_ctr[0] % 2
                    unit_ctr[0] += 1
                    kdefs = [(128, 0, (lambda g, half, s2=s2: k2c[s2].ap[:, (g * 2 + half) * 128:(g * 2 + half + 1) * 128]),
                              (lambda g, s2=s2: vvc[s2].ap[:, g * 128:(g + 1) * 128]), k2c[s2].keys + vvc[s2].keys),
                             (4, 1, (lambda g, half, bb=bb: K2T[:, g, half, bb * 4:(bb + 1) * 4]),
                              (lambda g, bb=bb: vpn[bb].ap[0:4, g * 128:(g + 1) * 128]), ["K2T"] + vpn[bb].keys)]
                    for kb, (nk, mi, kfn, vfn, rk) in enumerate(kdefs):
                        sbank = next_bank()

                        def fs(e, sbank=sbank, nk=nk, mi=mi, kfn=kfn, bb=bb):
                            mb = maskb[0:nk, mi, 0:4]
                            mrhs = bass.AP(mb.tensor, mb.offset, [list(mb.ap[0]), [0, 32], [1, 4]])
                            ins = e.matmul(psum[0:nk, sbank, 0:128], lhsT=identb[0:nk, 0:nk], rhs=mrhs, start=True, stop=False, skip_group_check=True)
                            for g in range(4):
                                for half in range(2):
                                    ins = e.matmul(psum[0:nk, sbank, half * 64 + 16 * g:half * 64 + 16 * g + 16], lhsT=kfn(g, half),
                                                   rhs=qT_all[:, 4 * g:4 * g + 4, bb * 4:(bb + 1) * 4], start=False,
                                                   stop=(g == 3 and half == 1), skip_group_check=True)
                            return ins
                        P.op("pe", fs, reads=["identb", "maskb"] + rk + q_keys, writes=[("ps", sbank)])
                        pt = Pt[u_s][kb]
                        P.op("act", lambda e, sbank=sbank, nk=nk, pt=pt: e.activation(out=pt.ap[0:nk, 0:128], in_=psum[0:nk, sbank, 0:128],
                                                                                    func=AF.Exp, scale=0.125),
                             writes=[("ps", sbank)] + pt.keys)
                    obank = next_bank()

                    def fo(e, obank=obank, kdefs=kdefs, u_s=u_s):
                        ins = None
                        for g in range(4):
                            for kb, (nk, mi, kfn, vfn, rk) in enumerate(kdefs):
                                pa = Pt[u_s][kb].ap[0:nk, 16 * g:16 * g + 16]
                                prhs = bass.AP(pa.tensor, pa.offset, [list(pa.ap[0]), [64, 2], [1, 16]])
                                oa = psum[:, obank, 16 * g:16 * g + 16]
                                oout = bass.AP(oa.tensor, oa.offset, [list(oa.ap[0]), [64, 2], [1, 16]])
                                ins = e.matmul(oout, lhsT=vfn(g), rhs=prhs, start=(kb == 0), stop=(kb == 1), skip_group_check=True)
                        for kb, (nk, mi, kfn, vfn, rk) in enumerate(kdefs):
                            ins = e.matmul(psum[:, obank, 256:384], lhsT=onesb[0:nk, :], rhs=Pt[u_s][kb].ap[0:nk, 0:128],
                                           start=(kb == 0), stop=(kb == 1), skip_group_check=True)
                        return ins
                    P.op("pe", fo, reads=["onesb"] + [kk for kb in range(2) for kk in Pt[u_s][kb].keys] + kdefs[0][4] + kdefs[1][4],
                         writes=[("ps", obank)])
                    dn = den[u_s]
                    for half in range(2):
                        rows = slice(half * 64, (half + 1) * 64)
                        ea = esk[rows, :]
                        esb = bass.AP(ea.tensor, ea.offset, [list(ea.ap[0]), [1, 16], [0, 4]])
                        P.op("dve", lambda e, obank=obank, rows=rows, half=half, esb=esb, dn=dn: e.tensor_tensor(
                            out=dn.ap[rows, 0:64].rearrange("p (c q) -> p c q", c=16),
                            in0=psum[rows, obank, 256 + half * 64:256 + (half + 1) * 64].rearrange("p (c q) -> p c q", c=16),
                            in1=esb, op=ALU.add), reads=CONST_R + dn.keys, writes=[("ps", obank)] + dn.keys)
                    P.op("dve", lambda e, dn=dn: e.reciprocal(out=dn.ap[:, 0:64], in_=dn.ap[:, 0:64]), reads=dn.keys, writes=dn.keys)
                    for half in range(2):
                        rows = slice(half * 64, (half + 1) * 64)
                        P.op("dve", lambda e, obank=obank, rows=rows, half=half, dn=dn, bb=bb: e.tensor_tensor(
                            out=oT_all[rows, :, bb * 4:(bb + 1) * 4],
                            in0=psum[rows, obank, half * 64:(half + 1) * 64].rearrange("p (c q) -> p c q", c=16),
                            in1=dn.ap[rows, 0:64].rearrange("p (c q) -> p c q", c=16), op=ALU.mult),
                            reads=dn.keys, writes=[("ps", obank)] + o_keys)

            gemm_to_h(T, lambda k: oT[k].ap, lambda k: oT[k].keys, post={"xg": V_GMLP1})
            if phase_limit < 6:
                return
            mlp(T, post={"xg": None})
            if phase_limit < 7:
                return
            xo = [UV(16 * 1024 + c * T * 4, F32, T) for c in range(NCH)]
            normalize(T, V_GFIN, lambda c: xo[c].ap, lambda c: xo[c].keys)
            if not is_s:
                for tb in range(4):
                    stg = io_stage[tb % 2]
                    transposes_out(lambda c, tb=tb: xo[c].ap[:, tb * 128:(tb + 1) * 128], lambda c: xo[c].keys, 128, stg.ap, stg.keys)
                    P.dma("sp", yp[pi * 512 + tb * 128:pi * 512 + (tb + 1) * 128, :], stg.ap, s_out[tb % 2], reads=stg.keys)
            else:
                stg = io_stage[0]
                transposes_out(lambda c: xo[c].ap, lambda c: xo[c].keys, 64, stg.ap, stg.keys)
                P.dma("sp", ys, stg.ap[0:64, :], s_out[0], reads=stg.keys)

        for pi in passes:
            run_pass(pi)
        if dump_h:
            s_dbg = P.new_dma_sem('s_dbg')
            P.dma('sp', dbg_h, hres[:], s_dbg, reads=[('h', c) for c in range(NCH)])
            P.dma('sp', dbg_u, U[:], s_dbg, reads=[('U', g) for g in range(64)])
            P.dma('sp', dbg_k, K2T[:].rearrange("p g h t -> p (g h t)"), s_dbg, reads=["K2T"])
            P.dma('sp', dbg_v, Vpad[:].rearrange("p b g d -> p (b g d)"), s_dbg, reads=[("Vpad", i) for i in range(5)])
        P.finalize()
        P.emit()
    return nc


def _fm(v):
    return np.ascontiguousarray(np.asarray(v, np.float32).reshape(16, 128).T)


def kernel(x_prompt, x_sample, state_conv, state_h, cache_k, cache_v, norm_mix, norm_mlp, rec_w_in,
           rec_conv_w, rec_conv_b, rec_gate_a_w, rec_gate_a_b, rec_gate_x_w, rec_gate_x_b, rec_lambda,
           rec_w_out, kv_norm, w_kv, attn_w_q, attn_sinks, attn_w_o, mlp_w_up, mlp_w_down, final_norm):
    f = lambda a: np.ascontiguousarray(np.asarray(a, dtype=np.float32))
    sinks_rep = np.repeat(np.asarray(attn_sinks, np.float32)[0].reshape(16, 2), 64, axis=1).T
    vec_list = [_fm(norm_mix[0]), _fm(norm_mlp[0]), _fm(kv_norm), _fm(norm_mix[1]), _fm(norm_mlp[1]), _fm(final_norm),
                _fm(rec_conv_w[0, 0]), _fm(rec_conv_w[0, 1]), _fm(rec_conv_w[0, 2]), _fm(rec_conv_w[0, 3]), _fm(rec_conv_b[0]),
                _fm(np.asarray(rec_gate_a_b)[0].reshape(-1)), _fm(np.asarray(rec_gate_x_b)[0].reshape(-1)), _fm(rec_lambda[0]),
                sinks_rep]
    vecs = np.ascontiguousarray(np.concatenate(vec_list, axis=1).astype(np.float32))
    shared = {
        "vecs": vecs, "w_in": f(rec_w_in[0]), "w_ga": f(rec_gate_a_w[0]), "w_gx": f(rec_gate_x_w[0]), "w_out": f(rec_w_out[0]),
        "w_kv": f(w_kv), "w_q": f(attn_w_q[0]), "w_o": f(attn_w_o[0]), "w_up0": f(mlp_w_up[0]), "w_up1": f(mlp_w_up[1]),
        "w_dn0": f(mlp_w_down[0]), "w_dn1": f(mlp_w_down[1]),
    }
    x_prompt = np.asarray(x_prompt, np.float32)
    x_sample = np.asarray(x_sample, np.float32)
    state_conv = np.asarray(state_conv, np.float32)
    state_h = np.asarray(state_h, np.float32)
    cache_k = np.asarray(cache_k, np.float32)
    cache_v = np.asarray(cache_v, np.float32)
    in_maps = []
    for i in range(NCORES):
        bs = slice(16 * i, 16 * (i + 1))
        m = dict(shared)
        m["xp"] = f(x_prompt[i])
        m["xs"] = f(x_sample[bs].reshape(64, D))
        m["sconv"] = f(state_conv[bs, 0].reshape(48, D))
        m["sh"] = f(state_h[bs, 0])
        m["ck"] = f(cache_k[bs].reshape(16, 128, 256))
        m["cv"] = f(cache_v[bs].reshape(16, 128, 256))
        in_maps.append(m)
    nc = build_program()
    res = run_bass_kernel_spmd(nc, in_maps, core_ids=list(range(NCORES)))
    R = res.results
    cat = lambda k: np.stack([np.asarray(r[k], np.float32) for r in R], 0)
    y_prompt = cat("yp")
    y_sample = cat("ys").reshape(128, 4, D)
    conv_p = cat("o_convp").reshape(8, 1, 3, D)
    h_p = cat("o_hp").reshape(8, 1, D)
    k_p = cat("o_kp").reshape(8, 128, 4, 64)
    v_p = cat("o_vp").reshape(8, 128, 4, 64)
    conv_s = cat("o_convs").reshape(128, 1, 3, D)
    h_s = cat("o_hs").reshape(128, 1, D)
    k_s = cat("o_ks").reshape(128, 4, 4, 64)
    v_s = cat("o_vs").reshape(128, 4, 4, 64)
    return (y_prompt, y_sample, conv_p, h_p, k_p, v_p, conv_s, h_s, k_s, v_s)
```

```python
import contextlib
import numpy as np
import concourse.bass as bass
import concourse.mybir as mybir
from concourse.bass_utils import run_bass_kernel_spmd

F32 = mybir.dt.float32
BF16 = mybir.dt.bfloat16
AF = mybir.ActivationFunctionType
ALU = mybir.AluOpType

ENGS = ("pe", "act", "dve", "pool", "sp")
NCORES = 8
D = 2048
NCH = 16
DFF = 8192
EPS = 1e-6
NEG = -30000.0
NSLOT = 3


class Sem:
    def __init__(self, handle, name):
        self.h = handle
        self.name = name
        self.count = 0


class Prog:
    def __init__(self, nc, stack):
        self.nc = nc
        self.stack = stack
        self.q = {e: [] for e in ENGS}
        self.esem = {e: self.new_sem("prog_" + e) for e in ENGS}
        self.seen = {e: {} for e in ENGS}
        self.res = {}
        self.dma_sems = []

    def new_sem(self, name):
        return Sem(self.stack.enter_context(self.nc.semaphore(name)), name)

    def new_dma_sem(self, name):
        s = self.new_sem(name)
        self.dma_sems.append(s)
        return s

    def _deps(self, eng, reads, writes, is_dma):
        need = {}

        def add(tok, raw):
            if tok is None:
                return
            sem, val, teng = tok
            if (not is_dma) and teng == eng:
                if not raw or eng == "pe":
                    return
            if need.get(sem, 0) < val:
                need[sem] = val

        for k in reads:
            st = self.res.get(k)
            if st is not None:
                add(st[0], True)
        for k in writes:
            st = self.res.get(k)
            if st is not None:
                add(st[0], False)
                for t in st[1]:
                    add(t, False)
        waits = []
        seen = self.seen[eng]
        for sem, val in need.items():
            if seen.get(sem, 0) >= val:
                continue
            seen[sem] = val
            waits.append((sem, val))
        return waits

    def _commit(self, tok, reads, writes):
        for k in reads:
            st = self.res.setdefault(k, [None, []])
            st[1].append(tok)
        for k in writes:
            self.res[k] = [tok, []]

    def op(self, eng, fn, reads=(), writes=()):
        reads = list(reads)
        writes = list(writes)
        waits = self._deps(eng, reads, writes, False)
        sem = self.esem[eng]
        sem.count += 1
        tok = (sem, sem.count, eng)
        self.q[eng].append((waits, fn, sem, 1))
        self._commit(tok, reads, writes)
        return tok

    def dma(self, eng, out, in_, sem, reads=(), writes=()):
        reads = list(reads)
        writes = list(writes)
        waits = self._deps(eng, reads, writes, True)
        sem.count += 16
        tok = (sem, sem.count, None)
        self.q[eng].append((waits, lambda e: e.dma_start(out=out, in_=in_), sem, 16))
        self._commit(tok, reads, writes)
        return tok

    def dma_group(self, eng, pairs, sem, reads=(), writes=()):
        reads = list(reads)
        writes = list(writes)
        waits = self._deps(eng, reads, writes, True)
        for i, (out, in_) in enumerate(pairs):
            sem.count += 16
            self.q[eng].append((waits if i == 0 else [], (lambda e, out=out, in_=in_: e.dma_start(out=out, in_=in_)), sem, 16))
        tok = (sem, sem.count, None)
        self._commit(tok, reads, writes)
        return tok

    def finalize(self):
        waits = [(s, s.count) for s in self.dma_sems if s.count > 0]
        self.q["sp"].append((waits, None, None, 0))

    def emit(self):
        nc = self.nc
        names = {"pe": "tensor", "act": "scalar", "dve": "vector", "pool": "gpsimd", "sp": "sync"}
        with nc.Block() as block:
            for eng in ENGS:
                items = self.q[eng]
                if not items:
                    continue

                def body(e, items=items):
                    for waits, fn, sem, inc in items:
                        for s, v in waits:
                            e.wait_ge(s.h, v)
                        if fn is None:
                            continue
                        ins = fn(e)
                        ins.then_inc(sem.h, inc)

                getattr(block, names[eng])(body)


V_GMIX0, V_GMLP0, V_GKV, V_GMIX1, V_GMLP1, V_GFIN, V_CW0, V_CW1, V_CW2, V_CW3, V_CB, V_BA, V_BX, V_LAM, V_SINK = range(15)
NVEC = 15


def build_program(passes=(0, 1, 2, 3, 4), phase_limit=99, dump_h=False):
    nc = bass.Bass("TRN2", target_bir_lowering=False)

    def din(name, shape):
        return nc.dram_tensor(name, shape, F32, kind="ExternalInput").ap()

    def dout(name, shape):
        return nc.dram_tensor(name, shape, F32, kind="ExternalOutput").ap()

    xp = din("xp", [2048, D])
    xs = din("xs", [64, D])
    sconv = din("sconv", [48, D])
    sh = din("sh", [16, D])
    ck = din("ck", [16, 128, 256])
    cv = din("cv", [16, 128, 256])
    vecs_d = din("vecs", [128, NVEC * 16])
    w_in = din("w_in", [D, 2 * D])
    w_ga = din("w_ga", [16, 128, 128])
    w_gx = din("w_gx", [16, 128, 128])
    w_out = din("w_out", [D, D])
    w_kv = din("w_kv", [D, 512])
    w_q = din("w_q", [D, D])
    w_o = din("w_o", [D, D])
    w_up = [din("w_up0", [D, DFF]), din("w_up1", [D, DFF])]
    w_dn = [din("w_dn0", [DFF, D]), din("w_dn1", [DFF, D])]

    yp = dout("yp", [2048, D])
    ys = dout("ys", [64, D])
    o_convp = dout("o_convp", [3, D])
    o_hp = dout("o_hp", [1, D])
    o_kp = dout("o_kp", [128, 256])
    o_vp = dout("o_vp", [128, 256])
    o_convs = dout("o_convs", [16, 3, D])
    o_hs = dout("o_hs", [16, D])
    o_ks = dout("o_ks", [64, 256])
    o_vs = dout("o_vs", [64, 256])
    wscr = nc.dram_tensor("wscr", [85, 128, 8192], BF16, kind="Internal").ap()
    dbg_h = dout("dbg_h", [128, NCH, 512]) if dump_h else None
    dbg_u = dout("dbg_u", [128, 16384]) if dump_h else None
    dbg_k = nc.dram_tensor("dbg_k", [128, 4 * 2 * 640], BF16, kind="ExternalOutput").ap() if dump_h else None
    dbg_v = nc.dram_tensor("dbg_v", [128, 5 * 4 * 192], BF16, kind="ExternalOutput").ap() if dump_h else None

    with contextlib.ExitStack() as st:
        P = Prog(nc, st)

        def sb(name, shape, dt):
            return st.enter_context(nc.sbuf_tensor("sb_" + name, shape, dt))

        hres = sb("hres", [128, NCH, 512], F32)
        xn = sb("xn", [128, NCH, 512], BF16)
        U = sb("U", [128, 16384], F32)
        wslot = [sb(f"wslot{i}", [128, 8192], BF16) for i in range(NSLOT)]
        wgate = sb("wgate", [128, 2, 16, 128], BF16)
        K2T = sb("K2T", [128, 4, 2, 640], BF16)
        Vpad = sb("Vpad", [128, 5, 4, 192], BF16)
        vecs = sb("vecs", [128, NVEC * 16], F32)
        dvec = sb("dvec", [128, 5 * 16], F32)
        ident = sb("ident", [128, 128], F32)
        identb = sb("identb", [128, 128], BF16)
        onesb = sb("onesb", [128, 128], BF16)
        onespad = sb("onespad", [128, 192], BF16)
        maskf = sb("maskf", [128, 2, 128], F32)
        maskb = sb("maskb", [128, 2, 128], BF16)
        rstd = sb("rstd", [128, 512], F32)
        rsq = sb("rsq", [128, 512], F32)
        relu_t = sb("relu_t", [128, 2, 512], F32)
        sqst = sb("sqst", [128, 4, 512], BF16)
        carry = sb("carry", [128, NCH, 4], F32)
        psum = st.enter_context(nc.psum_tensor("psum", [128, 8, 512], F32))

        s_in = [P.new_dma_sem(f"s_in{i}") for i in range(2)]
        s_out = [P.new_dma_sem(f"s_out{i}") for i in range(2)]
        s_w = [P.new_dma_sem(f"s_w{i}") for i in range(NSLOT)]
        s_ws = [P.new_dma_sem(f"s_ws{i}") for i in range(NSLOT)]
        s_misc = P.new_dma_sem("s_misc")
        s_sh = P.new_dma_sem("s_sh")
        s_ck = [P.new_dma_sem(f"s_ck{i}") for i in range(2)]
        s_cv = [P.new_dma_sem(f"s_cv{i}") for i in range(2)]
        s_kvo = [P.new_dma_sem(f"s_kvo{i}") for i in range(2)]
        s_st = [P.new_dma_sem(f"s_st{i}") for i in range(2)]

        class UV:
            def __init__(self, off, dt, n):
                esz = 4 if dt == F32 else 2
                assert off % 4 == 0 and (n * esz) % 4 == 0
                assert off + n * esz <= 65536, (off, n, esz)
                a = U[:, off // 4:(off + n * esz) // 4]
                self.ap = a if dt == F32 else a.bitcast(BF16)
                self.keys = [("U", g) for g in range(off // 1024, (off + n * esz - 1) // 1024 + 1)]

        bank_ctr = [0]

        held_banks = set()

        def next_bank():
            while True:
                b = bank_ctr[0] % 8
                bank_ctr[0] += 1
                if b not in held_banks:
                    return b

        def vcol(v, c):
            return vecs[:, v * 16 + c:v * 16 + c + 1]

        def dcol(v, c):
            return dvec[:, v * 16 + c:v * 16 + c + 1]

        DV_HBA, DV_HBX, DV_CL, DV_CL2, DV_ESINK = range(5)

        P.dma("sp", vecs[:], vecs_d, s_misc, writes=["vecs"])
        s_wg = P.new_dma_sem("s_wg")
        P.dma_group("pool", [(wgate[:, 0, :, :], w_ga.rearrange("n c d -> c n d")),
                             (wgate[:, 1, :, :], w_gx.rearrange("n c d -> c n d"))], s_wg, writes=["wgate0", "wgate1"])

        P.op("pool", lambda e: e.memset(ident[:], 0.0), writes=["ident"])
        P.op("pool", lambda e: e.affine_select(out=ident[:], in_=ident[:], pattern=[[-1, 128]],
                                               compare_op=ALU.not_equal, fill=1.0, base=0, channel_multiplier=1),
             reads=["ident"], writes=["ident"])
        P.op("pool", lambda e: e.memset(maskf[:], 0.0), writes=["maskf"])
        P.op("pool", lambda e: e.affine_select(out=maskf[:, 0, :], in_=maskf[:, 0, :], pattern=[[-1, 128]],
                                               compare_op=ALU.is_gt, fill=NEG, base=0, channel_multiplier=1),
             reads=["maskf"], writes=["maskf"])
        P.op("pool", lambda e: e.affine_select(out=maskf[:, 1, :], in_=maskf[:, 1, :], pattern=[[1, 128]],
                                               compare_op=ALU.is_ge, fill=NEG, base=0, channel_multiplier=-1),
             reads=["maskf"], writes=["maskf"])
        P.op("dve", lambda e: e.tensor_copy(out=maskb[:], in_=maskf[:]), reads=["maskf"], writes=["maskb"])
        P.op("dve", lambda e: e.tensor_copy(out=identb[:], in_=ident[:]), reads=["ident"], writes=["identb"])
        P.op("dve", lambda e: e.memset(onesb[:], 1.0), writes=["onesb"])
        P.op("dve", lambda e: e.memset(onespad[:], 0.0), writes=["onespad"])
        P.op("dve", lambda e: e.memset(onespad[:, 64:128], 1.0), reads=["onespad"], writes=["onespad"])
        P.op("dve", lambda e: e.memset(Vpad[:], 0.0), writes=[("Vpad", i) for i in range(5)])
        P.op("dve", lambda e: e.memset(K2T[:], 0.0), writes=["K2T"])
        P.op("dve", lambda e: e.memset(carry[:], 0.0), writes=[("carry", c) for c in range(NCH)])
        P.op("dve", lambda e: e.tensor_scalar(out=dvec[:, 0:32], in0=vecs[:, V_BA * 16:V_BA * 16 + 32], scalar1=0.5,
                                              scalar2=None, op0=ALU.mult), reads=["vecs"], writes=["dv_hb"])
        P.op("act", lambda e: e.activation(out=dvec[:, 32:48], in_=vecs[:, V_LAM * 16:V_LAM * 16 + 16], func=AF.Exp,
                                           scale=-1.0), reads=["vecs"], writes=["dv_t"])
        P.op("act", lambda e: e.activation(out=dvec[:, 48:64], in_=dvec[:, 32:48], func=AF.Ln, bias=1.0),
             reads=["dv_t"], writes=["dv_sp"])
        P.op("dve", lambda e: e.tensor_scalar(out=dvec[:, 32:48], in0=dvec[:, 48:64], scalar1=-8.0, scalar2=None,
                                              op0=ALU.mult), reads=["dv_sp"], writes=["dv_t"])
        P.op("dve", lambda e: e.tensor_scalar(out=dvec[:, 48:64], in0=dvec[:, 32:48], scalar1=0.5, scalar2=None,
                                              op0=ALU.mult), reads=["dv_t"], writes=["dv_sp"])
        P.op("act", lambda e: e.activation(out=dvec[:, 64:80], in_=vecs[:, V_SINK * 16:V_SINK * 16 + 16], func=AF.Exp),
             reads=["vecs"], writes=["dv_es"])
        CONST_R = ["vecs", "dv_hb", "dv_t", "dv_sp", "dv_es"]

        blocks = []

        def add_block(parts, kc, nct):
            blocks.append((parts, kc, nct))

        for pi in passes:
            for cp in range(8):
                add_block([(w_in[:, cp * 256:(cp + 1) * 256], 0, 256), (w_in[:, D + cp * 256:D + (cp + 1) * 256], 256, 256)], 16, 512)
            for b in range(4):
                add_block([(w_out[:, b * 512:(b + 1) * 512], 0, 512)], 16, 512)
            for b in range(16):
                add_block([(w_up[0][:, b * 512:(b + 1) * 512], 0, 512)], 16, 512)
            for b in range(16):
                add_block([(w_dn[0][:, b * 128:(b + 1) * 128], 0, 128)], 64, 128)
            add_block([(w_kv[:, :], 0, 512)], 16, 512)
            for b in range(4):
                add_block([(w_q[:, b * 512:(b + 1) * 512], 0, 512)], 16, 512)
            for b in range(4):
                add_block([(w_o[:, b * 512:(b + 1) * 512], 0, 512)], 16, 512)
            for b in range(16):
                add_block([(w_up[1][:, b * 512:(b + 1) * 512], 0, 512)], 16, 512)
            for b in range(16):
                add_block([(w_dn[1][:, b * 128:(b + 1) * 128], 0, 128)], 64, 128)

        wstate = {"issued": 0, "next": 0}

        NBLK = 85

        NSTORE = 1

        def conv_pass(bi):
            if len(passes) < 3:
                return 0
            return 0 if bi % 3 == 0 else 1

        def stored(bi):
            return True

        def w_issue(i):
            parts, kc, nct = blocks[i]
            s = i % NSLOT
            pord, bi = divmod(i, NBLK)
            if pord > conv_pass(bi):
                P.dma("pool", wslot[s][:, :], wscr[bi], s_w[s], reads=[("wscr", bi)], writes=[("ws", s, j) for j in range(4)])
                return
            view = wslot[s][:, 0:kc * nct].rearrange("p (k n) -> p k n", k=kc)
            dmas = []
            for (src, off, n) in parts:
                srcv = src.rearrange("(k p) n -> p k n", p=128)
                for k0 in range(0, kc, 16):
                    dmas.append((view[:, k0:k0 + 16, off:off + n], srcv[:, k0:k0 + 16, :]))
            assert len(dmas) <= 4
            P.dma_group("pool", dmas, s_w[s], writes=[("ws", s, j) for j in range(4)])

        def w_acquire():
            i = wstate["next"]
            wstate["next"] += 1
            while wstate["issued"] < min(len(blocks), i + NSLOT):
                w_issue(wstate["issued"])
                wstate["issued"] += 1
            parts, kc, nct = blocks[i]
            s = i % NSLOT
            pord, bi = divmod(i, NBLK)
            if pord == conv_pass(bi) and pord < len(passes) - 1:
                P.dma("sp", wscr[bi], wslot[s][:, :], s_ws[s], reads=[("ws", s, j) for j in range(4)], writes=[("wscr", bi)])
            view = wslot[s][:, 0:kc * nct].rearrange("p (k n) -> p k n", k=kc)
            return view, [("ws", s, j) for j in range(4)]

        def mm_group(out_ap, bank, pairs, reads):
            n = len(pairs)

            def fn(e):
                ins = None
                for i, (l, r) in enumerate(pairs):
                    ins = e.matmul(out_ap, lhsT=l, rhs=r, start=(i == 0), stop=(i == n - 1))
                return ins
            P.op("pe", fn, reads=reads, writes=[("ps", bank)])

        stats = {}

        def stats_begin():
            bank = next_bank()
            held_banks.add(bank)
            stats["bank"] = bank

        def stats_sq(T, c, sq_ap, sq_keys):
            P.op("act", lambda e: e.activation(out=sq_ap, in_=hres[:, c, 0:T], func=AF.Square), reads=[("h", c)], writes=sq_keys)

        def stats_mm(T, c, sq_ap, sq_keys):
            bank = stats["bank"]
            P.op("pe", lambda e: e.matmul(psum[:, bank, 0:T], lhsT=onesb[:], rhs=sq_ap, start=(c == 0), stop=(c == NCH - 1)),
                 reads=["onesb"] + sq_keys, writes=[("ps", bank)])

        def stats_chunk(T, c, sq_ap, sq_keys):
            stats_sq(T, c, sq_ap, sq_keys)
            stats_mm(T, c, sq_ap, sq_keys)

        def stats_finish(T):
            bank = stats["bank"]
            P.op("act", lambda e: e.activation(out=rsq[:, 0:T], in_=psum[:, bank, 0:T], func=AF.Sqrt,
                                               scale=1.0 / D, bias=EPS), writes=[("ps", bank), "rsq"])
            P.op("dve", lambda e: e.reciprocal(out=rstd[:, 0:T], in_=rsq[:, 0:T]), reads=["rsq"], writes=["rstd"])
            held_banks.discard(bank)

        def stats_classic(T):
            stats_begin()
            for c in range(NCH):
                stats_chunk(T, c, xn[:, c, 0:T], [("xn", c)])
            stats_finish(T)

        def normalize(T, gv, dst_chunk, dst_keys):
            for c in range(NCH):
                P.op("dve", lambda e, c=c: e.scalar_tensor_tensor(out=dst_chunk(c), in0=hres[:, c, 0:T], scalar=vcol(gv, c),
                                                                  in1=rstd[:, 0:T], op0=ALU.mult, op1=ALU.mult),
                     reads=[("h", c), "rstd"] + CONST_R, writes=dst_keys(c))

        def scale_g(T, gv, c, eng):
            if eng == "act":
                P.op("act", lambda e: e.activation(out=xn[:, c, 0:T], in_=hres[:, c, 0:T], func=AF.Copy, scale=vcol(gv, c)),
                     reads=[("h", c)] + CONST_R, writes=[("xn", c)])
            else:
                P.op("dve", lambda e: e.tensor_scalar(out=xn[:, c, 0:T], in0=hres[:, c, 0:T], scalar1=vcol(gv, c), scalar2=None,
                                                      op0=ALU.mult), reads=[("h", c)] + CONST_R, writes=[("xn", c)])

        DEFER = 2

        def post_chunk(T, m, post):
            if post is None:
                return
            if m == 0:
                stats_begin()
            stats_sq(T, m, sqst[:, m % 4, 0:T], [("sqst", m % 4)])
            if post.get("xg") is not None:
                scale_g(T, post["xg"], m, "act")
            if m - DEFER >= 0:
                mm = m - DEFER
                stats_mm(T, mm, sqst[:, mm % 4, 0:T], [("sqst", mm % 4)])
            if m == NCH - 1:
                for mm in range(NCH - DEFER, NCH):
                    stats_mm(T, mm, sqst[:, mm % 4, 0:T], [("sqst", mm % 4)])
                stats_finish(T)

        def gemm_to_h(T, rhs_chunk, rhs_keys, post=None):
            for b in range(4):
                view, wk = w_acquire()
                for j in range(4):
                    m = b * 4 + j
                    bank = next_bank()
                    mm_group(psum[:, bank, 0:T], bank, [(view[:, k, j * 128:(j + 1) * 128], rhs_chunk(k)) for k in range(NCH)],
                             reads=wk + [kk for k in range(NCH) for kk in rhs_keys(k)])
                    P.op("dve", lambda e, m=m, bank=bank: e.tensor_tensor(out=hres[:, m, 0:T], in0=hres[:, m, 0:T],
                                                                         in1=psum[:, bank, 0:T], op=ALU.add),
                         reads=[("h", m)], writes=[("ps", bank), ("h", m)])
                    post_chunk(T, m, post)

        def mlp(T, post):
            hid = [UV(j * T * 2, BF16, T) for j in range(64)]
            for b in range(16):
                view, wk = w_acquire()
                for j in range(4):
                    m = b * 4 + j
                    bank = next_bank()
                    mm_group(psum[:, bank, 0:T], bank, [(view[:, k, j * 128:(j + 1) * 128], xn[:, k, 0:T]) for k in range(NCH)],
                             reads=wk + [("xn", k) for k in range(NCH)])
                    r = m % 2
                    P.op("dve", lambda e, r=r, bank=bank: e.scalar_tensor_tensor(out=relu_t[:, r, 0:T], in0=psum[:, bank, 0:T], scalar=0.0,
                                                                                in1=rstd[:, 0:T], op0=ALU.max, op1=ALU.mult),
                         reads=["rstd"], writes=[("ps", bank), ("relu", r)])
                    P.op("act", lambda e, r=r, m=m: e.activation(out=hid[m].ap, in_=relu_t[:, r, 0:T], func=AF.Square),
                         reads=[("relu", r)], writes=hid[m].keys)
            for m in range(16):
                view, wk = w_acquire()
                bank = next_bank()
                mm_group(psum[:, bank, 0:T], bank, [(view[:, k, :], hid[k].ap) for k in range(64)],
                         reads=wk + [kk for k in range(64) for kk in hid[k].keys])
                P.op("dve", lambda e, m=m, bank=bank: e.tensor_tensor(out=hres[:, m, 0:T], in0=hres[:, m, 0:T],
                                                                     in1=psum[:, bank, 0:T], op=ALU.add),
                     reads=[("h", m)], writes=[("ps", bank), ("h", m)])
                post_chunk(T, m, post)

        def transposes_out(src_chunk, src_keys, ncols, stage, stage_keys):
            for cg in range(4):
                bank = next_bank()

                def fn(e, cg=cg, bank=bank):
                    ins = None
                    for j in range(4):
                        ins = e.transpose(out=psum[0:ncols, bank, j * 128:(j + 1) * 128], in_=src_chunk(cg * 4 + j), identity=ident[:])
                    return ins
                P.op("pe", fn, reads=["ident"] + [kk for j in range(4) for kk in src_keys(cg * 4 + j)], writes=[("ps", bank)])
                eng = "act" if cg % 2 else "dve"
                if eng == "act":
                    P.op("act", lambda e, cg=cg, bank=bank: e.activation(out=stage[0:ncols, cg * 512:(cg + 1) * 512],
                                                                        in_=psum[0:ncols, bank, :], func=AF.Copy),
                         writes=[("ps", bank)] + stage_keys)
                else:
                    P.op("dve", lambda e, cg=cg, bank=bank: e.tensor_copy(out=stage[0:ncols, cg * 512:(cg + 1) * 512],
                                                                         in_=psum[0:ncols, bank, :]),
                         writes=[("ps", bank)] + stage_keys)

        x_prefetched = set()

        def run_pass(pi):
            is_s = (pi == 4)
            T = 64 if is_s else 512
            last_p = (pi == 3)
            io_stage = [UV(i * 8192, F32, 2048) for i in range(2)]
            in_stage = [UV(48 * 1024 + i * 8192, F32, 2048) for i in range(2)]

            def tv(ap):
                return ap.rearrange("p (b t) -> p b t", t=4) if is_s else ap

            if not is_s:
                for tb in range(4):
                    sg = in_stage[tb % 2]
                    if (pi, tb) not in x_prefetched:
                        P.dma("sp", sg.ap, xp[pi * 512 + tb * 128:pi * 512 + (tb + 1) * 128, :], s_in[tb % 2], writes=sg.keys)
                    for cg in range(4):
                        bank = next_bank()

                        def fn(e, cg=cg, bank=bank, sg=sg):
                            ins = None
                            for j in range(4):
                                c = cg * 4 + j
                                ins = e.transpose(out=psum[:, bank, j * 128:(j + 1) * 128], in_=sg.ap[:, c * 128:(c + 1) * 128],
                                                  identity=ident[:])
                            return ins
                        P.op("pe", fn, reads=["ident"] + sg.keys, writes=[("ps", bank)])
                        wr = [("h", cg * 4 + j) for j in range(4)]
                        if cg % 2:
                            P.op("act", lambda e, cg=cg, bank=bank, tb=tb: e.activation(
                                out=hres[:, cg * 4:(cg + 1) * 4, tb * 128:(tb + 1) * 128],
                                in_=psum[:, bank, :].rearrange("p (j t) -> p j t", j=4), func=AF.Copy), writes=[("ps", bank)] + wr)
                        else:
                            P.op("dve", lambda e, cg=cg, bank=bank, tb=tb: e.tensor_copy(
                                out=hres[:, cg * 4:(cg + 1) * 4, tb * 128:(tb + 1) * 128],
                                in_=psum[:, bank, :].rearrange("p (j t) -> p j t", j=4)), writes=[("ps", bank)] + wr)
                    cols = slice(tb * 128, (tb + 1) * 128)
                    if tb == 0:
                        stats_begin()
                    sbank = stats["bank"]
                    for cg in range(4):
                        P.op("act", lambda e, cg=cg, cols=cols: e.activation(out=xn[:, cg * 4:(cg + 1) * 4, cols],
                                                                            in_=hres[:, cg * 4:(cg + 1) * 4, cols], func=AF.Square),
                             reads=[("h", cg * 4 + j) for j in range(4)], writes=[("xn", cg * 4 + j) for j in range(4)])

                    def fn_s(e, cols=cols, sbank=sbank):
                        ins = None
                        for c in range(NCH):
                            ins = e.matmul(psum[:, sbank, cols], lhsT=onesb[:], rhs=xn[:, c, cols], start=(c == 0), stop=(c == NCH - 1),
                                           skip_group_check=True)
                        return ins
                    P.op("pe", fn_s, reads=["onesb"] + [("xn", c) for c in range(NCH)], writes=[("ps", sbank)])
                    P.op("act", lambda e, cols=cols, sbank=sbank: e.activation(out=rsq[:, cols], in_=psum[:, sbank, cols], func=AF.Sqrt,
                                                                              scale=1.0 / D, bias=EPS), writes=[("ps", sbank), "rsq"])
                    P.op("dve", lambda e, cols=cols: e.reciprocal(out=rstd[:, cols], in_=rsq[:, cols]), reads=["rsq"], writes=["rstd"])
                    for c in range(NCH):
                        P.op("dve", lambda e, c=c, cols=cols: e.scalar_tensor_tensor(out=xn[:, c, cols], in0=hres[:, c, cols],
                                                                                    scalar=vcol(V_GMIX0, c), in1=rstd[:, cols],
                                                                                    op0=ALU.mult, op1=ALU.mult),
                             reads=[("h", c), "rstd"] + CONST_R, writes=[("xn", c)])
                    if tb == 3:
                        held_banks.discard(sbank)
            else:
                sg = io_stage[0]
                P.dma("sp", sg.ap[0:64, :], xs, s_in[0], writes=sg.keys)
                for cg in range(4):
                    bank = next_bank()

                    def fn(e, cg=cg, bank=bank, sg=sg):
                        ins = None
                        for j in range(4):
                            c = cg * 4 + j
                            ins = e.transpose(out=psum[:, bank, j * 64:(j + 1) * 64], in_=sg.ap[0:64, c * 128:(c + 1) * 128],
                                              identity=ident[0:64, 0:64])
                        return ins
                    P.op("pe", fn, reads=["ident"] + sg.keys, writes=[("ps", bank)])
                    P.op("dve", lambda e, cg=cg, bank=bank: e.tensor_copy(
                        out=hres[:, cg * 4:(cg + 1) * 4, 0:64], in_=psum[:, bank, 0:256].rearrange("p (j t) -> p j t", j=4)),
                        writes=[("ps", bank)] + [("h", cg * 4 + j) for j in range(4)])
                sconvT = UV(48 * 1024, F32, 16 * 48)
                h0s = UV(51 * 1024, F32, 16 * 16)
                sg1 = io_stage[1]
                P.dma("sp", sg1.ap[0:48, :], sconv, s_in[1], writes=sg1.keys)
                for cg in range(4):
                    bank = next_bank()

                    def fn(e, cg=cg, bank=bank):
                        ins = None
                        for j in range(4):
                            c = cg * 4 + j
                            ins = e.transpose(out=psum[:, bank, j * 48:(j + 1) * 48], in_=sg1.ap[0:48, c * 128:(c + 1) * 128],
                                              identity=ident[0:48, 0:48])
                        return ins
                    P.op("pe", fn, reads=["ident"] + sg1.keys, writes=[("ps", bank)])
                    P.op("dve", lambda e, cg=cg, bank=bank: e.tensor_copy(out=sconvT.ap[:, cg * 192:(cg + 1) * 192],
                                                                         in_=psum[:, bank, 0:192]),
                         writes=[("ps", bank)] + sconvT.keys)
                sg2 = UV(32 * 1024, F32, 2048)
                P.dma("sp", sg2.ap[0:16, :], sh, s_sh, writes=sg2.keys)
                for cg in range(4):
                    bank = next_bank()

                    def fn(e, cg=cg, bank=bank):
                        ins = None
                        for j in range(4):
                            c = cg * 4 + j
                            ins = e.transpose(out=psum[:, bank, j * 16:(j + 1) * 16], in_=sg2.ap[0:16, c * 128:(c + 1) * 128],
                                              identity=ident[0:16, 0:16])
                        return ins
                    P.op("pe", fn, reads=["ident"] + sg2.keys, writes=[("ps", bank)])
                    P.op("dve", lambda e, cg=cg, bank=bank: e.tensor_copy(out=h0s.ap[:, cg * 64:(cg + 1) * 64],
                                                                         in_=psum[:, bank, 0:64]),
                         writes=[("ps", bank)] + h0s.keys)

            if phase_limit < 1:
                return
            if is_s:
                stats_classic(T)
                normalize(T, V_GMIX0, lambda c: xn[:, c, 0:T], lambda c: [("xn", c)])
            yoff = 52 * 1024 if is_s else 0
            y_in = [UV(yoff + c * T * 2, BF16, T) for c in range(NCH)]
            TW = 512 if is_s else 2048
            tbase = 16 * 1024
            SLOT = {"gg": (0, 4), "xc": (4, 3), "thi": (7, 3), "a": (10, 3), "sq": (13, 3), "thr": (16, 2), "u": (18, 1),
                    "hs": (19, 1), "xcb": (20, 1)}

            def tmp(name, c, n=None, dt=F32):
                base, nb = SLOT[name]
                if name == "xcb":
                    return UV(tbase + base * TW + (c % 2) * (TW // 2), BF16, T)
                return UV(tbase + (base + c % nb) * TW, dt, n if n is not None else T)

            def xre_buf(c):
                return UV((58 * 1024 if not is_s else 56 * 1024) + (c % 2) * 2304, F32, NXE)
            if is_s:
                xr_all = [UV(40 * 1024 + c * 256, F32, 64) for c in range(NCH)]
                hs_all = [UV(44 * 1024 + c * 256, F32, 64) for c in range(NCH)]
            NXE = 112 if is_s else 515

            rec_w = {}

            def st_A_pe(c):
                if c % 2 == 0:
                    rec_w["view"], rec_w["wk"] = w_acquire()
                view, wk = rec_w["view"], rec_w["wk"]
                jj = c % 2
                b1 = next_bank()
                mm_group(psum[:, b1, 0:T], b1, [(view[:, k, jj * 128:(jj + 1) * 128], xn[:, k, 0:T]) for k in range(NCH)],
                         reads=wk + [("xn", k) for k in range(NCH)])
                b2 = next_bank()
                mm_group(psum[:, b2, 0:T], b2, [(view[:, k, 256 + jj * 128:256 + (jj + 1) * 128], xn[:, k, 0:T]) for k in range(NCH)],
                         reads=wk + [("xn", k) for k in range(NCH)])
                rec_w[("banks", c)] = (b1, b2)

            def st_A_evac(c):
                b1, b2 = rec_w[("banks", c)]
                gg = tmp("gg", c)
                xre = xre_buf(c)
                P.op("act", lambda e: e.activation(out=gg.ap, in_=psum[:, b1, 0:T], func=AF.Gelu_apprx_tanh),
                     writes=[("ps", b1)] + gg.keys)
                if is_s:
                    xv = xre.ap.rearrange("p (b s) -> p b s", s=7)
                    P.op("dve", lambda e: e.tensor_copy(out=xv[:, :, 3:7], in_=psum[:, b2, 0:64].rearrange("p (b t) -> p b t", t=4)),
                         writes=[("ps", b2)] + xre.keys)
                    P.op("dve", lambda e: e.tensor_copy(out=xv[:, :, 0:3],
                                                        in_=sconvT.ap[:, c * 48:(c + 1) * 48].rearrange("p (b j) -> p b j", j=3)),
                         reads=sconvT.keys, writes=xre.keys)
                    P.op("act", lambda e: e.activation(out=xr_all[c].ap, in_=psum[:, b2, 0:64], func=AF.Copy),
                         writes=[("ps", b2)] + xr_all[c].keys)
                else:
                    P.op("dve", lambda e: e.tensor_copy(out=xre.ap[:, 3:515], in_=psum[:, b2, 0:512]),
                         writes=[("ps", b2)] + xre.keys)
                    P.op("dve", lambda e: e.tensor_copy(out=xre.ap[:, 0:3], in_=carry[:, c, 0:3]),
                         reads=[("carry", c)], writes=xre.keys)

            def st_B1_dve(c):
                xre = xre_buf(c)
                xc = tmp("xc", c)
                xcb = tmp("xcb", c)
                if is_s:
                    xv = xre.ap.rearrange("p (b s) -> p b s", s=7)
                    sh_ = lambda j: xv[:, :, j:j + 4]
                else:
                    sh_ = lambda j: xre.ap[:, j:j + 512]
                P.op("dve", lambda e: e.tensor_scalar(out=tv(xc.ap), in0=sh_(0), scalar1=vcol(V_CW0, c), scalar2=vcol(V_CB, c),
                                                      op0=ALU.mult, op1=ALU.add), reads=xre.keys + CONST_R, writes=xc.keys)
                for j in range(1, 4):
                    P.op("dve", lambda e, j=j: e.scalar_tensor_tensor(out=tv(xc.ap), in0=sh_(j), scalar=vcol(V_CW0 + j, c), in1=tv(xc.ap),
                                                                      op0=ALU.mult, op1=ALU.add),
                         reads=xre.keys + xc.keys + CONST_R, writes=xc.keys)
                if not is_s:
                    P.op("dve", lambda e: e.tensor_copy(out=carry[:, c, 0:3], in_=xre.ap[:, 512:515]), reads=xre.keys,
                         writes=[("carry", c)])
                P.op("dve", lambda e: e.tensor_copy(out=xcb.ap, in_=xc.ap), reads=xc.keys, writes=xcb.keys)

            def st_B1_pe(c):
                xcb = tmp("xcb", c)
                b1 = next_bank()
                mm_group(psum[:, b1, 0:T], b1, [(wgate[:, 0, c, :], xcb.ap)], reads=["wgate0"] + xcb.keys)
                b2 = next_bank()
                mm_group(psum[:, b2, 0:T], b2, [(wgate[:, 1, c, :], xcb.ap)], reads=["wgate1"] + xcb.keys)
                rec_w[("gbanks", c)] = (b1, b2)

            def st_B1_act(c):
                b1, b2 = rec_w[("gbanks", c)]
                thr, thi, a, sq = tmp("thr", c), tmp("thi", c), tmp("a", c), tmp("sq", c)
                P.op("act", lambda e: e.activation(out=thr.ap, in_=psum[:, b1, 0:T], func=AF.Tanh, scale=0.5, bias=dcol(DV_HBA, c)),
                     reads=CONST_R, writes=[("ps", b1)] + thr.keys)
                P.op("act", lambda e: e.activation(out=thi.ap, in_=psum[:, b2, 0:T], func=AF.Tanh, scale=0.5, bias=dcol(DV_HBX, c)),
                     reads=CONST_R, writes=[("ps", b2)] + thi.keys)
                P.op("act", lambda e: e.activation(out=a.ap, in_=thr.ap, func=AF.Exp, scale=dcol(DV_CL2, c), bias=dcol(DV_CL2, c)),
                     reads=thr.keys + CONST_R, writes=a.keys)
                P.op("act", lambda e: e.activation(out=sq.ap, in_=thr.ap, func=AF.Exp, scale=dcol(DV_CL, c), bias=dcol(DV_CL, c)),
                     reads=thr.keys + CONST_R, writes=sq.keys)
                P.op("act", lambda e: e.activation(out=sq.ap, in_=sq.ap, func=AF.Sqrt, scale=-0.25, bias=0.25),
                     reads=sq.keys, writes=sq.keys)

            def st_B2(c):
                gg, xc, thi, a, sq = tmp("gg", c), tmp("xc", c), tmp("thi", c), tmp("a", c), tmp("sq", c)
                u, hs = tmp("u", c), tmp("hs", c)
                P.op("dve", lambda e: e.scalar_tensor_tensor(out=u.ap, in0=thi.ap, scalar=1.0, in1=xc.ap, op0=ALU.add, op1=ALU.mult),
                     reads=thi.keys + xc.keys, writes=u.keys)
                P.op("dve", lambda e: e.tensor_tensor(out=u.ap, in0=u.ap, in1=sq.ap, op=ALU.mult), reads=u.keys + sq.keys, writes=u.keys)
                if not is_s:
                    P.op("dve", lambda e: e.tensor_tensor_scan(out=hs.ap, data0=a.ap, data1=u.ap, initial=carry[:, c, 3:4],
                                                               op0=ALU.mult, op1=ALU.add),
                         reads=a.keys + u.keys + [("carry", c)], writes=hs.keys)
                    P.op("dve", lambda e: e.tensor_copy(out=carry[:, c, 3:4], in_=hs.ap[:, 511:512]), reads=hs.keys,
                         writes=[("carry", c)])
                else:
                    av, uv, hv = tv(a.ap), tv(u.ap), tv(hs.ap)
                    for t in range(4):
                        prev = h0s.ap[:, c * 16:(c + 1) * 16] if t == 0 else hv[:, :, t - 1]
                        P.op("dve", lambda e, t=t, prev=prev: e.tensor_tensor(out=hv[:, :, t], in0=av[:, :, t], in1=prev, op=ALU.mult),
                             reads=a.keys + hs.keys + h0s.keys, writes=hs.keys)
                        P.op("dve", lambda e, t=t: e.tensor_tensor(out=hv[:, :, t], in0=hv[:, :, t], in1=uv[:, :, t], op=ALU.add),
                             reads=u.keys + hs.keys, writes=hs.keys)
                    P.op("act", lambda e: e.activation(out=hs_all[c].ap, in_=hs.ap, func=AF.Copy), reads=hs.keys, writes=hs_all[c].keys)
                P.op("dve", lambda e: e.tensor_tensor(out=y_in[c].ap, in0=gg.ap, in1=hs.ap, op=ALU.mult),
                     reads=gg.keys + hs.keys, writes=y_in[c].keys)

            for i in range(NCH + 3):
                if i < NCH:
                    st_A_pe(i)
                if 0 <= i - 3 < NCH:
                    st_B2(i - 3)
                if 0 <= i - 1 < NCH:
                    st_B1_dve(i - 1)
                    st_B1_pe(i - 1)
                if i < NCH:
                    st_A_evac(i)
                if 0 <= i - 1 < NCH:
                    st_B1_act(i - 1)

            if phase_limit < 2:
                return
            gemm_to_h(T, lambda k: y_in[k].ap, lambda k: y_in[k].keys, post={"xg": V_GMLP0})
            if last_p:
                stg = io_stage[0]
                transposes_out(lambda c: carry[:, c, :], lambda c: [("carry", c)], 4, stg.ap, stg.keys)
                P.dma_group("sp", [(o_convp, stg.ap[0:3, :]), (o_hp, stg.ap[3:4, :])], s_st[0], reads=stg.keys)
            if is_s:
                stg = io_stage[0]
                transposes_out(lambda c: xr_all[c].ap, lambda c: xr_all[c].keys, 64, stg.ap, stg.keys)
                ps_ = stg.ap.ap[0][0]
                P.dma_group("sp", [(o_convs[:, j, :], bass.AP(stg.ap.tensor, stg.ap.offset + (1 + j) * ps_, [[4 * ps_, 16], [1, 2048]]))
                                   for j in range(3)], s_st[0], reads=stg.keys)
                stg1 = io_stage[1]
                transposes_out(lambda c: hs_all[c].ap, lambda c: hs_all[c].keys, 64, stg1.ap, stg1.keys)
                ps1 = stg1.ap.ap[0][0]
                src = bass.AP(stg1.ap.tensor, stg1.ap.offset + 3 * ps1, [[4 * ps1, 16], [1, 2048]])
                P.dma("sp", o_hs, src, s_st[1], reads=stg1.keys)

            if phase_limit < 3:
                return
            mlp(T, post={"xg": V_GKV})

            if phase_limit < 4:
                return
            rtm = UV(60 * 1024, F32, 16)
            bank = next_bank()
            ntr = 16 if is_s else 4
            wtr = 4 if is_s else 128

            def fn_r(e, bank=bank):
                ins = None
                for i in range(ntr):
                    ins = e.transpose(out=psum[0:wtr, bank, i:i + 1], in_=rstd[0:1, i * wtr:(i + 1) * wtr], identity=ident[0:1, 0:1])
                return ins
            P.op("pe", fn_r, reads=["ident", "rstd"], writes=[("ps", bank)])
            P.op("dve", lambda e, bank=bank: e.tensor_copy(out=rtm.ap[0:wtr, 0:ntr], in_=psum[0:wtr, bank, 0:ntr]),
                 writes=[("ps", bank)] + rtm.keys)
            view, wk = w_acquire()
            koff = 0 if is_s else 128
            for g in range(4):
                bank = next_bank()
                n = NCH

                def fn(e, g=g, bank=bank, view=view):
                    ins = None
                    for half in range(2):
                        for k in range(n):
                            ins = e.matmul(psum[half * 64:(half + 1) * 64, bank, 0:T], lhsT=view[:, k, g * 64:(g + 1) * 64],
                                           rhs=xn[:, k, 0:T], start=(k == 0), stop=(k == n - 1))
                    return ins
                P.op("pe", fn, reads=wk + [("xn", k) for k in range(NCH)], writes=[("ps", bank)])
                for half in range(2):
                    P.op("dve", lambda e, g=g, bank=bank, half=half: e.tensor_tensor(
                        out=K2T[half * 64:(half + 1) * 64, g, half, koff:koff + T], in0=psum[half * 64:(half + 1) * 64, bank, 0:T],
                        in1=rstd[half * 64:(half + 1) * 64, 0:T], op=ALU.mult), reads=["rstd"], writes=[("ps", bank), "K2T"])
            if not is_s:
                for tb in range(4):
                    bank = next_bank()
                    mm_group(psum[:, bank, :], bank, [(xn[:, k, tb * 128:(tb + 1) * 128], view[:, k, :]) for k in range(NCH)],
                             reads=wk + [("xn", k) for k in range(NCH)])
                    P.op("dve", lambda e, tb=tb, bank=bank: e.tensor_scalar(
                        out=Vpad[:, tb + 1, :, 64:128], in0=psum[:, bank, 256:512].rearrange("p (g d) -> p g d", g=4),
                        scalar1=rtm.ap[:, tb:tb + 1], scalar2=None, op0=ALU.mult),
                        reads=rtm.keys, writes=[("ps", bank), ("Vpad", tb + 1)])
                    if last_p and tb == 3:
                        kvo = UV(40 * 1024, F32, 512)
                        P.op("act", lambda e, bank=bank, tb=tb: e.activation(out=kvo.ap, in_=psum[:, bank, :], func=AF.Copy,
                                                                            scale=rtm.ap[:, tb:tb + 1]),
                             reads=rtm.keys, writes=[("ps", bank)] + kvo.keys)
                        P.dma_group("sp", [(o_kp, kvo.ap[:, 0:256]), (o_vp, kvo.ap[:, 256:512])], s_kvo[0], reads=kvo.keys)
            else:
                vpn = [UV(40 * 1024 + i * 1024, BF16, 512) for i in range(16)]
                kvos = [UV(24 * 1024 + i * 2048, F32, 512) for i in range(2)]
                for bb in range(16):
                    bank = next_bank()
                    mm_group(psum[0:4, bank, :], bank, [(xn[:, k, bb * 4:(bb + 1) * 4], view[:, k, :]) for k in range(NCH)],
                             reads=wk + [("xn", k) for k in range(NCH)])
                    pv = psum[0:4, bank, 256:512]
                    vsrc = bass.AP(pv.tensor, pv.offset, [list(pv.ap[0]), [64, 4], [0, 2], [1, 64]])
                    P.op("dve", lambda e, bb=bb, vsrc=vsrc: e.tensor_scalar(
                        out=vpn[bb].ap[0:4, :].rearrange("p (g r d) -> p g r d", g=4, r=2), in0=vsrc,
                        scalar1=rtm.ap[0:4, bb:bb + 1], scalar2=None, op0=ALU.mult),
                        reads=rtm.keys, writes=[("ps", bank)] + vpn[bb].keys)
                    kvo = kvos[bb % 2]
                    P.op("act", lambda e, bank=bank, kvo=kvo, bb=bb: e.activation(out=kvo.ap[0:4, :], in_=psum[0:4, bank, :], func=AF.Copy,
                                                                                scale=rtm.ap[0:4, bb:bb + 1]),
                         reads=rtm.keys, writes=[("ps", bank)] + kvo.keys)
                    P.dma_group("sp", [(o_ks[bb * 4:(bb + 1) * 4, :], kvo.ap[0:4, 0:256]),
                                       (o_vs[bb * 4:(bb + 1) * 4, :], kvo.ap[0:4, 256:512])], s_kvo[bb % 2], reads=kvo.keys)

            if phase_limit < 5:
                return
            for c in range(NCH):
                scale_g(T, V_GMIX1, c, "act" if c % 2 else "dve")
            qT = [UV(c * T * 2, BF16, T) for c in range(NCH)]
            oT = [UV(NCH * T * 2 + c * T * 2, BF16, T) for c in range(NCH)]
            for b in range(4):
                view, wk = w_acquire()
                for j in range(4):
                    m = b * 4 + j
                    bank = next_bank()
                    mm_group(psum[:, bank, 0:T], bank, [(view[:, k, j * 128:(j + 1) * 128], xn[:, k, 0:T]) for k in range(NCH)],
                             reads=wk + [("xn", k) for k in range(NCH)])
                    P.op("dve", lambda e, m=m, bank=bank: e.tensor_tensor(out=qT[m].ap, in0=psum[:, bank, 0:T], in1=rstd[:, 0:T], op=ALU.mult),
                         reads=["rstd"], writes=[("ps", bank)] + qT[m].keys)

            Pt = [[UV(32 * 1024 + (s * 2 + kb) * 1024, BF16, 512) for kb in range(2)] for s in range(2)]
            den = [UV(36 * 1024 + s * 1024, F32, 256) for s in range(2)]
            unit_ctr = [0]

            def attn_unit(chunks, nq, qsl, kblocks):
                s = unit_ctr[0] % 2
                unit_ctr[0] += 1
                ncol = 2 * len(chunks) * nq
                wo = len(chunks) * nq
                for kb, (nk, kfn, vfn, mi, rk) in enumerate(kblocks):
                    bank = next_bank()

                    def fn(e, bank=bank, nk=nk, kfn=kfn, mi=mi):
                        mb = maskb[0:nk, mi, 0:nq]
                        mrhs = bass.AP(mb.tensor, mb.offset, [list(mb.ap[0]), [0, 2 * len(chunks)], [1, nq]])
                        ins = e.matmul(psum[0:nk, bank, 0:ncol], lhsT=identb[0:nk, 0:nk], rhs=mrhs, start=True, stop=False, skip_group_check=True)
                        for ci, c in enumerate(chunks):
                            g = c // 4
                            for half in range(2):
                                j = 2 * ci + half
                                ins = e.matmul(psum[0:nk, bank, j * nq:(j + 1) * nq], lhsT=kfn(g, half),
                                               rhs=qT[c].ap[:, qsl], start=False,
                                               stop=(ci == len(chunks) - 1 and half == 1), skip_group_check=True)
                        return ins
                    P.op("pe", fn, reads=["identb", "maskb"] + rk + [kk for c in chunks for kk in qT[c].keys], writes=[("ps", bank)])
                    pt = Pt[s][kb]
                    P.op("act", lambda e, bank=bank, nk=nk, pt=pt: e.activation(out=pt.ap[0:nk, 0:ncol], in_=psum[0:nk, bank, 0:ncol],
                                                                              func=AF.Exp, scale=0.125),
                         writes=[("ps", bank)] + pt.keys)
                bank = next_bank()

                def fn2(e, bank=bank):
                    ins = None
                    for region, off in ((0, 0), (1, 256)):
                        for ci, c in enumerate(chunks):
                            g = c // 4
                            first = True
                            nmm = 2 * len(kblocks)
                            cnt = 0
                            for kb, (nk, kfn, vfn, mi, rk) in enumerate(kblocks):
                                for half in range(2):
                                    j = 2 * ci + half
                                    if region == 0:
                                        vp = vfn(g)
                                    else:
                                        vp = onespad[0:nk, :]
                                    lhsT = vp[:, 64:192] if half == 0 else vp[:, 0:128]
                                    cnt += 1
                                    ins = e.matmul(psum[:, bank, off + ci * nq:off + (ci + 1) * nq], lhsT=lhsT,
                                                   rhs=Pt[s][kb].ap[0:nk, j * nq:(j + 1) * nq], start=first, stop=(cnt == nmm),
                                                   skip_group_check=True)
                                    first = False
                    return ins
                P.op("pe", fn2, reads=["onespad"] + [kk for kb in range(len(kblocks)) for kk in Pt[s][kb].keys] +
                     [kk for (_, _, _, _, rk) in kblocks for kk in rk], writes=[("ps", bank)])
                c0 = chunks[0]
                nch = len(chunks)
                dn = den[s]
                es = dvec[:, DV_ESINK * 16 + c0:DV_ESINK * 16 + c0 + nch]
                esb = bass.AP(es.tensor, es.offset, [list(es.ap[0]), [1, nch], [0, nq]])
                P.op("dve", lambda e, bank=bank: e.tensor_tensor(out=dn.ap[:, 0:wo].rearrange("p (c q) -> p c q", c=nch),
                                                                 in0=psum[:, bank, 256:256 + wo].rearrange("p (c q) -> p c q", c=nch),
                                                                 in1=esb, op=ALU.add),
                     reads=CONST_R, writes=[("ps", bank)] + dn.keys)
                P.op("dve", lambda e: e.reciprocal(out=dn.ap[:, 0:wo], in_=dn.ap[:, 0:wo]), reads=dn.keys, writes=dn.keys)
                for ci, c in enumerate(chunks):
                    P.op("dve", lambda e, ci=ci, c=c, bank=bank: e.tensor_tensor(out=oT[c].ap[:, qsl], in0=psum[:, bank, ci * nq:(ci + 1) * nq],
                                                                                in1=dn.ap[:, ci * nq:(ci + 1) * nq], op=ALU.mult),
                         reads=dn.keys, writes=[("ps", bank)] + oT[c].keys)

            if not is_s:
                for qb in range(4):
                    qsl = slice(qb * 128, (qb + 1) * 128)
                    kbl = []
                    if not (pi == 0 and qb == 0):
                        kbl.append((128, (lambda g, half, qb=qb: K2T[:, g, half, qb * 128:(qb + 1) * 128]),
                                    (lambda g, qb=qb: Vpad[:, qb, g, :]), 0, ["K2T", ("Vpad", qb)]))
                    kbl.append((128, (lambda g, half, qb=qb: K2T[:, g, half, (qb + 1) * 128:(qb + 2) * 128]),
                                (lambda g, qb=qb: Vpad[:, qb + 1, g, :]), 1, ["K2T", ("Vpad", qb + 1)]))
                    for g in range(4):
                        for cp in range(2):
                            attn_unit([4 * g + 2 * cp, 4 * g + 2 * cp + 1], 128, qsl, kbl)
                K2v = K2T[:].rearrange("p g h t -> p (g h) t")
                P.op("dve", lambda e: e.tensor_copy(out=K2v[:, :, 0:128], in_=K2v[:, :, 512:640]), reads=["K2T"], writes=["K2T"])
                P.op("dve", lambda e: e.tensor_copy(out=Vpad[:, 0, :, :], in_=Vpad[:, 4, :, :]), reads=[("Vpad", 4)], writes=[("Vpad", 0)])
            else:
                ckst = [UV(4 * 1024 + i * 2048, F32, 512) for i in range(2)]
                vcst = [UV(8 * 1024 + i * 1024, F32, 256) for i in range(2)]
                k2c = [UV(10 * 1024 + i * 2048, BF16, 1024) for i in range(2)]
                vvc = [UV(14 * 1024 + i * 1024, BF16, 512) for i in range(2)]
                qT_all = UV(0, BF16, NCH * T).ap.rearrange("p (c t) -> p c t", c=NCH)
                oT_all = UV(NCH * T * 2, BF16, NCH * T).ap.rearrange("p (c t) -> p c t", c=NCH)
                q_keys = [kk for c in range(NCH) for kk in qT[c].keys]
                o_keys = [kk for c in range(NCH) for kk in oT[c].keys]
                for i in range(2):
                    P.op("dve", lambda e, i=i: e.memset(k2c[i].ap, 0.0), writes=k2c[i].keys)
                esk = dvec[:, DV_ESINK * 16:DV_ESINK * 16 + 16]
                for bb in range(16):
                    s2 = bb % 2
                    ckv = ckst[s2].ap.rearrange("p (g r d) -> p g r d", g=4, r=2)
                    src = ck[bb].rearrange("t (g d) -> t g d", g=4)
                    P.dma_group("sp", [(ckv[:, :, 0, :], src), (ckv[:, :, 1, :], src)], s_ck[s2],
                                writes=ckst[s2].keys + [("ckdup", s2)])
                    P.dma("sp", vcst[s2].ap, cv[bb], s_cv[s2], writes=vcst[s2].keys)
                    bank = next_bank()

                    def fn(e, bank=bank, s2=s2):
                        ins = None
                        for g in range(4):
                            ins = e.transpose(out=psum[:, bank, g * 128:(g + 1) * 128], in_=ckst[s2].ap[:, g * 128:(g + 1) * 128],
                                              identity=ident[:])
                        return ins
                    P.op("pe", fn, reads=["ident", ("ckdup", s2)] + ckst[s2].keys, writes=[("ps", bank)])
                    for half in range(2):
                        P.op("act", lambda e, bank=bank, s2=s2, half=half: e.activation(
                            out=k2c[s2].ap.rearrange("p (g h t) -> p g h t", g=4, h=2)[half * 64:(half + 1) * 64, :, half, :],
                            in_=psum[half * 64:(half + 1) * 64, bank, :].rearrange("p (g t) -> p g t", g=4), func=AF.Copy),
                            reads=k2c[s2].keys, writes=[("ps", bank)] + k2c[s2].keys)
                    vc = vcst[s2].ap
                    vcsrc = bass.AP(vc.tensor, vc.offset, [list(vc.ap[0]), [64, 4], [0, 2], [1, 64]])
                    P.op("dve", lambda e, s2=s2, vcsrc=vcsrc: e.tensor_copy(
                        out=vvc[s2].ap.rearrange("p (g r d) -> p g r d", g=4, r=2), in_=vcsrc),
                        reads=vcst[s2].keys, writes=vvc[s2].keys)
                    u_s = unit_ctr[0] % 2
                    unit_ctr[0] += 1
                    kdefs = [(128, 0, (lambda g, half, s2=s2: k2c[s2].ap[:, (g * 2 + half) * 128:(g * 2 + half + 1) * 128]),
                              (lambda g, s2=s2: vvc[s2].ap[:, g * 128:(g + 1) * 128]), k2c[s2].keys + vvc[s2].keys),
                             (4, 1, (lambda g, half, bb=bb: K2T[:, g, half, bb * 4:(bb + 1) * 4]),
                              (lambda g, bb=bb: vpn[bb].ap[0:4, g * 128:(g + 1) * 128]), ["K2T"] + vpn[bb].keys)]
                    for kb, (nk, mi, kfn, vfn, rk) in enumerate(kdefs):
                        sbank = next_bank()

                        def fs(e, sbank=sbank, nk=nk, mi=mi, kfn=kfn, bb=bb):
                            mb = maskb[0:nk, mi, 0:4]
                            mrhs = bass.AP(mb.tensor, mb.offset, [list(mb.ap[0]), [0, 32], [1, 4]])
                            ins = e.matmul(psum[0:nk, sbank, 0:128], lhsT=identb[0:nk, 0:nk], rhs=mrhs, start=True, stop=False, skip_group_check=True)
                            for g in range(4):
                                for half in range(2):
                                    ins = e.matmul(psum[0:nk, sbank, half * 64 + 16 * g:half * 64 + 16 * g + 16], lhsT=kfn(g, half),
                                                   rhs=qT_all[:, 4 * g:4 * g + 4, bb * 4:(bb + 1) * 4], start=False,
                                                   stop=(g == 3 and half == 1), skip_group_check=True)
                            return ins
                        P.op("pe", fs, reads=["identb", "maskb"] + rk + q_keys, writes=[("ps", sbank)])
                        pt = Pt[u_s][kb]
                        P.op("act", lambda e, sbank=sbank, nk=nk, pt=pt: e.activation(out=pt.ap[0:nk, 0:128], in_=psum[0:nk, sbank, 0:128],
                                                                                    func=AF.Exp, scale=0.125),
                             writes=[("ps", sbank)] + pt.keys)
                    obank = next_bank()

                    def fo(e, obank=obank, kdefs=kdefs, u_s=u_s):
                        ins = None
                        for g in range(4):
                            for kb, (nk, mi, kfn, vfn, rk) in enumerate(kdefs):
                                pa = Pt[u_s][kb].ap[0:nk, 16 * g:16 * g + 16]
                                prhs = bass.AP(pa.tensor, pa.offset, [list(pa.ap[0]), [64, 2], [1, 16]])
                                oa = psum[:, obank, 16 * g:16 * g + 16]
                                oout = bass.AP(oa.tensor, oa.offset, [list(oa.ap[0]), [64, 2], [1, 16]])
                                ins = e.matmul(oout, lhsT=vfn(g), rhs=prhs, start=(kb == 0), stop=(kb == 1), skip_group_check=True)
                        for kb, (nk, mi, kfn, vfn, rk) in enumerate(kdefs):
                            ins = e.matmul(psum[:, obank, 256:384], lhsT=onesb[0:nk, :], rhs=Pt[u_s][kb].ap[0:nk, 0:128],
                                           start=(kb == 0), stop=(kb == 1), skip_group_check=True)
                        return ins
                    P.op("pe", fo, reads=["onesb"] + [kk for kb in range(2) for kk in Pt[u_s][kb].keys] + kdefs[0][4] + kdefs[1][4],
                         writes=[("ps", obank)])
                    dn = den[u_s]
                    for half in range(2):
                        rows = slice(half * 64, (half + 1) * 64)
                        ea = esk[rows, :]
                        esb = bass.AP(ea.tensor, ea.offset, [list(ea.ap[0]), [1, 16], [0, 4]])
                        P.op("dve", lambda e, obank=obank, rows=rows, half=half, esb=esb, dn=dn: e.tensor_tensor(
                            out=dn.ap[rows, 0:64].rearrange("p (c q) -> p c q", c=16),
                            in0=psum[rows, obank, 256 + half * 64:256 + (half + 1) * 64].rearrange("p (c q) -> p c q", c=16),
                            in1=esb, op=ALU.add), reads=CONST_R + dn.keys, writes=[("ps", obank)] + dn.keys)
                    P.op("dve", lambda e, dn=dn: e.reciprocal(out=dn.ap[:, 0:64], in_=dn.ap[:, 0:64]), reads=dn.keys, writes=dn.keys)
                    for half in range(2):
                        rows = slice(half * 64, (half + 1) * 64)
                        P.op("dve", lambda e, obank=obank, rows=rows, half=half, dn=dn, bb=bb: e.tensor_tensor(
                            out=oT_all[rows, :, bb * 4:(bb + 1) * 4],
                            in0=psum[rows, obank, half * 64:(half + 1) * 64].rearrange("p (c q) -> p c q", c=16),
                            in1=dn.ap[rows, 0:64].rearrange("p (c q) -> p c q", c=16), op=ALU.mult),
                            reads=dn.keys, writes=[("ps", obank)] + o_keys)

            gemm_to_h(T, lambda k: oT[k].ap, lambda k: oT[k].keys, post={"xg": V_GMLP1})
            if phase_limit < 6:
                return
            mlp(T, post={"xg": None})
            if phase_limit < 7:
                return
            xo = [UV(16 * 1024 + c * T * 4, F32, T) for c in range(NCH)]
            nxt = passes[passes.index(pi) + 1] if passes.index(pi) + 1 < len(passes) else None
            if nxt is not None and nxt < 4:
                for tb in range(2):
                    P.dma("sp", in_stage[tb].ap, xp[nxt * 512 + tb * 128:nxt * 512 + (tb + 1) * 128, :], s_in[tb], writes=in_stage[tb].keys)
                    x_prefetched.add((nxt, tb))
            normalize(T, V_GFIN, lambda c: xo[c].ap, lambda c: xo[c].keys)
            if not is_s:
                for tb in range(4):
                    stg = io_stage[tb % 2]
                    transposes_out(lambda c, tb=tb: xo[c].ap[:, tb * 128:(tb + 1) * 128], lambda c: xo[c].keys, 128, stg.ap, stg.keys)
                    P.dma("sp", yp[pi * 512 + tb * 128:pi * 512 + (tb + 1) * 128, :], stg.ap, s_out[tb % 2], reads=stg.keys)
            else:
                stg = io_stage[0]
                transposes_out(lambda c: xo[c].ap, lambda c: xo[c].keys, 64, stg.ap, stg.keys)
                P.dma("sp", ys, stg.ap[0:64, :], s_out[0], reads=stg.keys)

        for pi in passes:
            run_pass(pi)
        if dump_h:
            s_dbg = P.new_dma_sem('s_dbg')
            P.dma('sp', dbg_h, hres[:], s_dbg, reads=[('h', c) for c in range(NCH)])
            P.dma('sp', dbg_u, U[:], s_dbg, reads=[('U', g) for g in range(64)])
            P.dma('sp', dbg_k, K2T[:].rearrange("p g h t -> p (g h t)"), s_dbg, reads=["K2T"])
            P.dma('sp', dbg_v, Vpad[:].rearrange("p b g d -> p (b g d)"), s_dbg, reads=[("Vpad", i) for i in range(5)])
        P.finalize()
        P.emit()
    return nc


def _fm(v):
    return np.ascontiguousarray(np.asarray(v, np.float32).reshape(16, 128).T)


def kernel(x_prompt, x_sample, state_conv, state_h, cache_k, cache_v, norm_mix, norm_mlp, rec_w_in,
           rec_conv_w, rec_conv_b, rec_gate_a_w, rec_gate_a_b, rec_gate_x_w, rec_gate_x_b, rec_lambda,
           rec_w_out, kv_norm, w_kv, attn_w_q, attn_sinks, attn_w_o, mlp_w_up, mlp_w_down, final_norm):
    f = lambda a: np.ascontiguousarray(np.asarray(a, dtype=np.float32))
    sinks_rep = np.repeat(np.asarray(attn_sinks, np.float32)[0].reshape(16, 2), 64, axis=1).T
    vec_list = [_fm(norm_mix[0]), _fm(norm_mlp[0]), _fm(kv_norm), _fm(norm_mix[1]), _fm(norm_mlp[1]), _fm(final_norm),
                _fm(rec_conv_w[0, 0]), _fm(rec_conv_w[0, 1]), _fm(rec_conv_w[0, 2]), _fm(rec_conv_w[0, 3]), _fm(rec_conv_b[0]),
                _fm(np.asarray(rec_gate_a_b)[0].reshape(-1)), _fm(np.asarray(rec_gate_x_b)[0].reshape(-1)), _fm(rec_lambda[0]),
                sinks_rep]
    vecs = np.ascontiguousarray(np.concatenate(vec_list, axis=1).astype(np.float32))
    shared = {
        "vecs": vecs, "w_in": f(rec_w_in[0]), "w_ga": f(rec_gate_a_w[0]), "w_gx": f(rec_gate_x_w[0]), "w_out": f(rec_w_out[0]),
        "w_kv": f(w_kv), "w_q": f(attn_w_q[0]), "w_o": f(attn_w_o[0]), "w_up0": f(mlp_w_up[0]), "w_up1": f(mlp_w_up[1]),
        "w_dn0": f(mlp_w_down[0]), "w_dn1": f(mlp_w_down[1]),
    }
    x_prompt = np.asarray(x_prompt, np.float32)
    x_sample = np.asarray(x_sample, np.float32)
    state_conv = np.asarray(state_conv, np.float32)
    state_h = np.asarray(state_h, np.float32)
    cache_k = np.asarray(cache_k, np.float32)
    cache_v = np.asarray(cache_v, np.float32)
    in_maps = []
    for i in range(NCORES):
        bs = slice(16 * i, 16 * (i + 1))
        m = dict(shared)
        m["xp"] = f(x_prompt[i])
        m["xs"] = f(x_sample[bs].reshape(64, D))
        m["sconv"] = f(state_conv[bs, 0].reshape(48, D))
        m["sh"] = f(state_h[bs, 0])
        m["ck"] = f(cache_k[bs].reshape(16, 128, 256))
        m["cv"] = f(cache_v[bs].reshape(16, 128, 256))
        in_maps.append(m)
    nc = build_program()
    res = run_bass_kernel_spmd(nc, in_maps, core_ids=list(range(NCORES)))
    R = res.results
    cat = lambda k: np.stack([np.asarray(r[k], np.float32) for r in R], 0)
    y_prompt = cat("yp")
    y_sample = cat("ys").reshape(128, 4, D)
    conv_p = cat("o_convp").reshape(8, 1, 3, D)
    h_p = cat("o_hp").reshape(8, 1, D)
    k_p = cat("o_kp").reshape(8, 128, 4, 64)
    v_p = cat("o_vp").reshape(8, 128, 4, 64)
    conv_s = cat("o_convs").reshape(128, 1, 3, D)
    h_s = cat("o_hs").reshape(128, 1, D)
    k_s = cat("o_ks").reshape(128, 4, 4, 64)
    v_s = cat("o_vs").reshape(128, 4, 4, 64)
    return (y_prompt, y_sample, conv_p, h_p, k_p, v_p, conv_s, h_s, k_s, v_s)
```
